# Optimizing a Trainium2 kernel written in Bass

```python
import math
import jax, jax.numpy as jnp
from jax import lax
import numpy as np

D_MODEL = 1024
BATCH = 2
SEQ = 8192
DEPTH = 4
DEC_BATCH = 32
DEC_SEQ = 32
PAST_LEN = 4096

CHUNK = 64
D_A = D_MODEL
CONV_A_WIDTH = 31
D_INNER = 2 * D_MODEL
SSM_HEAD_DIM = 64
SSM_HEADS = D_INNER // SSM_HEAD_DIM
SSM_GROUPS = 4
SSM_STATE = 128
CONV_B_WIDTH = 4
D_XBC = D_INNER + 2 * SSM_GROUPS * SSM_STATE
PLE_DIM = 256
IN_SIZES = (D_A, D_A, D_A, D_INNER, D_XBC, SSM_HEADS, D_MODEL, D_MODEL, D_MODEL)
D_IN_TOTAL = sum(IN_SIZES)
DEEPNORM_ALPHA = (2 * DEPTH) ** 0.25
DEEPNORM_BETA = (8 * DEPTH) ** -0.25
LN_EPS = 1e-5

kernel_name = "conformer_mamba2_gated_stream_step"


def layer_norm(x, g, b):
    xf = x.astype(jnp.float32)
    mu = jnp.mean(xf, axis=-1, keepdims=True)
    var = jnp.mean(jnp.square(xf - mu), axis=-1, keepdims=True)
    return ((xf - mu) * lax.rsqrt(var + LN_EPS) * g + b).astype(x.dtype)


def gated_rmsnorm(y, z, w):
    bsz, L, _ = y.shape
    g = (y * jax.nn.silu(z.astype(jnp.float32))).reshape(bsz, L, SSM_GROUPS, -1)
    g = g * lax.rsqrt(jnp.mean(jnp.square(g), axis=-1, keepdims=True) + LN_EPS)
    return (g.reshape(bsz, L, D_INNER) * w).astype(z.dtype)


def causal_dwconv(u, buf, w, bias):
    K, C = w.shape
    up = jnp.concatenate([buf.astype(u.dtype), u], axis=1)
    y = lax.conv_general_dilated(up, w[:, None, :].astype(u.dtype), window_strides=(1,),
                                 padding='VALID', dimension_numbers=('NWC', 'WIO', 'NWC'),
                                 feature_group_count=C)
    return y + bias, up[:, up.shape[1] - (K - 1):]


def ssd_scan(xh, dt, A, Bm, Cm, h0):
    bsz, L = xh.shape[:2]
    nc = -(-L // CHUNK)
    pad = nc * CHUNK - L
    hg = SSM_HEADS // SSM_GROUPS

    def padt(t):
        return jnp.pad(t, [(0, 0), (0, pad)] + [(0, 0)] * (t.ndim - 2))

    xdt = padt(xh * dt[..., None]).reshape(bsz, nc, CHUNK, SSM_GROUPS, hg, SSM_HEAD_DIM)
    a = padt(dt * A).reshape(bsz, nc, CHUNK, SSM_GROUPS, hg)
    Bc = padt(Bm).reshape(bsz, nc, CHUNK, SSM_GROUPS, SSM_STATE)
    Cc = padt(Cm).reshape(bsz, nc, CHUNK, SSM_GROUPS, SSM_STATE)
    a_cum = jnp.cumsum(a, axis=2)
    mask = jnp.tril(jnp.ones((CHUNK, CHUNK), dtype=bool))[:, :, None, None]
    seg = a_cum[:, :, :, None] - a_cum[:, :, None, :]
    Lm = jnp.exp(jnp.where(mask, seg, -jnp.inf))
    CB = jnp.einsum('bclgn,bcsgn->bclsg', Cc, Bc)
    y_diag = jnp.einsum('bclsg,bclsgh,bcsghp->bclghp', CB, Lm, xdt)
    decay_states = jnp.exp(a_cum[:, :, -1:] - a_cum)
    states = jnp.einsum('bclgn,bclgh,bclghp->bcghpn', Bc, decay_states, xdt)
    chunk_decay = jnp.exp(a_cum[:, :, -1])

    def step(h, inp):
        dec, st = inp
        return h * dec[..., None, None] + st, h

    h_init = h0.reshape(bsz, SSM_GROUPS, hg, SSM_HEAD_DIM, SSM_STATE)
    h_final, h_starts = lax.scan(step, h_init, (jnp.swapaxes(chunk_decay, 0, 1), jnp.swapaxes(states, 0, 1)))
    h_starts = jnp.swapaxes(h_starts, 0, 1)
    y_off = jnp.einsum('bclgn,bcghpn,bclgh->bclghp', Cc, h_starts, jnp.exp(a_cum))
    y = (y_diag + y_off).reshape(bsz, nc * CHUNK, SSM_HEADS, SSM_HEAD_DIM)[:, :L]
    return y, h_final.reshape(bsz, SSM_HEADS, SSM_HEAD_DIM, SSM_STATE)


def trunk_layer(x, p, buf_a, buf_b, h0, w_in, b_in, conv_a_w, conv_a_b, norm_a_g, norm_a_b, w_a_out,
                conv_b_w, conv_b_b, dt_bias, a_log, d_skip, gnorm_w, w_b_out, w_out, w_ple, ln_g, ln_b):
    bsz, L, _ = x.shape
    proj = x @ w_in + b_in
    idx = np.cumsum(IN_SIZES)[:-1].tolist()
    a_val, a_glu, a_gate, z, xbc, dt_raw, g_a, g_b, g_p = jnp.split(proj, idx, axis=-1)

    u = a_val * jax.nn.sigmoid(a_glu)
    u, new_buf_a = causal_dwconv(u, buf_a, conv_a_w, conv_a_b)
    u = jax.nn.silu(layer_norm(u, norm_a_g, norm_a_b)) * jax.nn.silu(a_gate)
    out_a = u @ w_a_out

    xbc, new_buf_b = causal_dwconv(xbc, buf_b, conv_b_w, conv_b_b)
    xbc = jax.nn.silu(xbc)
    xs, Bm, Cm = jnp.split(xbc, [D_INNER, D_INNER + SSM_GROUPS * SSM_STATE], axis=-1)
    xs_h = xs.reshape(bsz, L, SSM_HEADS, SSM_HEAD_DIM).astype(jnp.float32)
    dt = jax.nn.softplus(dt_raw.astype(jnp.float32) + dt_bias.astype(jnp.float32))
    A = -jnp.exp(a_log.astype(jnp.float32))
    y, h_new = ssd_scan(xs_h, dt, A,
                        Bm.reshape(bsz, L, SSM_GROUPS, SSM_STATE).astype(jnp.float32),
                        Cm.reshape(bsz, L, SSM_GROUPS, SSM_STATE).astype(jnp.float32),
                        h0.astype(jnp.float32))
    y = (y + d_skip.astype(jnp.float32)[:, None] * xs_h).reshape(bsz, L, D_INNER)
    out_b = gated_rmsnorm(y, z, gnorm_w) @ w_b_out

    merged = jax.nn.sigmoid(g_a) * out_a + jax.nn.sigmoid(g_b) * out_b
    ple = jax.nn.sigmoid(g_p) * (p.astype(x.dtype) @ w_ple)
    x_new = layer_norm(DEEPNORM_ALPHA * x + merged @ w_out + ple, ln_g, ln_b)
    return x_new, new_buf_a, new_buf_b, h_new


def run_trunk(x, p, bufs_a, bufs_b, hs, weights):
    new_a, new_b, new_h = [], [], []
    for i in range(DEPTH):
        x, ba, bb, h = trunk_layer(x, p[i], bufs_a[i], bufs_b[i], hs[i], *[w[i] for w in weights])
        new_a.append(ba)
        new_b.append(bb)
        new_h.append(h)
    return x, jnp.stack(new_a), jnp.stack(new_b), jnp.stack(new_h)


def setup_inputs(seed: int = 0) -> dict:
    key = jax.random.key(seed)
    ks = jax.random.split(key, 32)
    f32 = jnp.float32
    nrm = lambda k, shape, s: (jax.random.normal(k, shape, f32) * s)
    dt0 = jnp.exp(jax.random.uniform(ks[0], (DEPTH, SSM_HEADS), f32) * (math.log(0.1) - math.log(0.001)) + math.log(0.001))
    return {
        "x_prompt": nrm(ks[1], (BATCH, SEQ, D_MODEL), 1.0),
        "x_sample": nrm(ks[2], (DEC_BATCH, DEC_SEQ, D_MODEL), 1.0),
        "cache_conv_a": nrm(ks[3], (DEPTH, DEC_BATCH, CONV_A_WIDTH - 1, D_A), 0.5),
        "cache_conv_b": nrm(ks[4], (DEPTH, DEC_BATCH, CONV_B_WIDTH - 1, D_XBC), 0.5),
        "state_ssm": nrm(ks[5], (DEPTH, DEC_BATCH, SSM_HEADS, SSM_HEAD_DIM, SSM_STATE), 0.1),
        "p_prompt": nrm(ks[6], (DEPTH, BATCH, SEQ, PLE_DIM), 1.0),
        "p_sample": nrm(ks[7], (DEPTH, DEC_BATCH, DEC_SEQ, PLE_DIM), 1.0),
        "w_in": nrm(ks[8], (DEPTH, D_MODEL, D_IN_TOTAL), D_MODEL ** -0.5),
        "b_in": nrm(ks[9], (DEPTH, D_IN_TOTAL), 0.01),
        "conv_a_w": nrm(ks[10], (DEPTH, CONV_A_WIDTH, D_A), CONV_A_WIDTH ** -0.5),
        "conv_a_b": nrm(ks[11], (DEPTH, D_A), 0.01),
        "norm_a_g": 1.0 + nrm(ks[12], (DEPTH, D_A), 0.02),
        "norm_a_b": nrm(ks[13], (DEPTH, D_A), 0.01),
        "w_a_out": nrm(ks[14], (DEPTH, D_A, D_MODEL), D_A ** -0.5 * DEEPNORM_BETA),
        "conv_b_w": nrm(ks[15], (DEPTH, CONV_B_WIDTH, D_XBC), CONV_B_WIDTH ** -0.5),
        "conv_b_b": nrm(ks[16], (DEPTH, D_XBC), 0.01),
        "dt_bias": dt0 + jnp.log(-jnp.expm1(-dt0)),
        "a_log": jnp.log(jax.random.uniform(ks[17], (DEPTH, SSM_HEADS), f32, 1.0, 16.0)),
        "d_skip": 1.0 + nrm(ks[18], (DEPTH, SSM_HEADS), 0.02),
        "gnorm_w": 1.0 + nrm(ks[19], (DEPTH, D_INNER), 0.02),
        "w_b_out": nrm(ks[20], (DEPTH, D_INNER, D_MODEL), D_INNER ** -0.5 * DEEPNORM_BETA),
        "w_out": nrm(ks[21], (DEPTH, D_MODEL, D_MODEL), D_MODEL ** -0.5 * DEEPNORM_BETA),
        "w_ple": nrm(ks[22], (DEPTH, PLE_DIM, D_MODEL), PLE_DIM ** -0.5 * DEEPNORM_BETA),
        "ln_g": 1.0 + nrm(ks[23], (DEPTH, D_MODEL), 0.02),
        "ln_b": nrm(ks[24], (DEPTH, D_MODEL), 0.01),
    }


def reference(x_prompt, x_sample, cache_conv_a, cache_conv_b, state_ssm, p_prompt, p_sample,
              w_in, b_in, conv_a_w, conv_a_b, norm_a_g, norm_a_b, w_a_out,
              conv_b_w, conv_b_b, dt_bias, a_log, d_skip, gnorm_w, w_b_out, w_out, w_ple, ln_g, ln_b):
    weights = (w_in, b_in, conv_a_w, conv_a_b, norm_a_g, norm_a_b, w_a_out,
               conv_b_w, conv_b_b, dt_bias, a_log, d_skip, gnorm_w, w_b_out, w_out, w_ple, ln_g, ln_b)
    bp = x_prompt.shape[0]
    zeros_a = jnp.zeros((DEPTH, bp, CONV_A_WIDTH - 1, D_A), x_prompt.dtype)
    zeros_b = jnp.zeros((DEPTH, bp, CONV_B_WIDTH - 1, D_XBC), x_prompt.dtype)
    zeros_h = jnp.zeros((DEPTH, bp, SSM_HEADS, SSM_HEAD_DIM, SSM_STATE), jnp.float32)
    y_prompt, conv_a_prompt, conv_b_prompt, ssm_prompt = run_trunk(x_prompt, p_prompt, zeros_a, zeros_b, zeros_h, weights)
    y_sample, conv_a_sample, conv_b_sample, ssm_sample = run_trunk(x_sample, p_sample, cache_conv_a, cache_conv_b, state_ssm, weights)
    return (y_prompt, y_sample, conv_a_prompt, conv_b_prompt, ssm_prompt, conv_a_sample, conv_b_sample, ssm_sample)
```

```python
import numpy as np
import os
from contextlib import ExitStack
import concourse.bass as bass
import concourse.mybir as mybir
from concourse.bass_utils import run_bass_kernel_spmd

F32 = mybir.dt.float32
BF16 = mybir.dt.bfloat16
AF = mybir.ActivationFunctionType
ALU = mybir.AluOpType

D = 1024
DEPTH = 4
DA = 1024
DI = 2048
NH = 32
HP = 64
NG = 4
NS = 128
DXBC = 3072
PLE = 256
DIN = 11296
KA = 31
KB = 4
ALPHA = float((2 * DEPTH) ** 0.25)
EPS = 1e-5
C_VAL, C_GLU, C_GATE, C_Z, C_XBC, C_DT, C_GA, C_GB, C_GP = 0, 1024, 2048, 3072, 5120, 8192, 8224, 9248, 10272
BI_VAL, BI_GLU, BI_GATE, BI_Z, BI_XBC, BI_GA, BI_GB, BI_GP = 0, 8, 16, 24, 40, 64, 72, 80

KSTOP = float(os.environ.get("KSTOP", "99"))
INTERLEAVE = os.environ.get("KINTER", "1") == "1"
PREFETCH = os.environ.get("KPREF", "1") == "1"


class _Stop(Exception):
    pass


def stage(k):
    if k > KSTOP:
        raise _Stop()


COMPUTE = ("pe", "act", "dve", "pool")
ALLQ = COMPUTE + ("sp",)


class Buf:
    __slots__ = ("name", "w", "r", "excl")

    def __init__(self, name, excl=False):
        self.name = name
        self.w = None
        self.r = []
        self.excl = excl


class Op:
    __slots__ = ("eng", "fn", "deps", "is_dma", "needed", "idx", "semval", "sem", "cc")

    def __init__(self, eng, fn, is_dma):
        self.eng = eng
        self.fn = fn
        self.deps = set()
        self.is_dma = is_dma
        self.cc = False
        self.needed = False
        self.semval = None
        self.sem = None


class Sched:
    def __init__(self, n_dma_sems=16, max_sem=32000):
        self.ops = []
        self.n_dma_sems = n_dma_sems
        self.max_sem = max_sem

    def add(self, eng, fn, reads=(), writes=(), dma=False, cc=False):
        dma = dma or cc
        op = Op(eng, fn, dma)
        op.cc = cc
        op.idx = len(self.ops)
        writes = list(writes) + [b for b in reads if b.excl]
        reads = [b for b in reads if not b.excl]
        for b in reads:
            if b.w is not None:
                op.deps.add(b.w)
        for b in writes:
            if b.w is not None:
                op.deps.add(b.w)
            for r in b.r:
                op.deps.add(r)
        for b in reads:
            if not dma:
                b.r = [r for r in b.r if self.ops[r].is_dma or self.ops[r].eng != eng]
            b.r.append(op.idx)
        for b in writes:
            b.w = op.idx
            b.r = []
        self.ops.append(op)
        return op

    def emit(self, semctx):
        ops = self.ops
        for op in ops:
            keep = set()
            for d in op.deps:
                p = ops[d]
                if d == op.idx:
                    continue
                if p.eng == "pe" and op.eng == "pe" and not p.is_dma and not op.is_dma:
                    continue
                keep.add(d)
            op.deps = keep
        eng_sem_idx = {e: 0 for e in ALLQ}
        eng_cnt = {e: 0 for e in ALLQ}
        dma_rr = {e: 0 for e in ALLQ}
        dma_cnt = {}
        dma_gen = {}
        prev_dma_on_sem = {}
        for op in ops:
            if op.cc:
                op.sem = f"cc_{op.idx}"
                op.semval = 1
            elif op.is_dma:
                k = dma_rr[op.eng] % self.n_dma_sems
                dma_rr[op.eng] += 1
                base = f"d_{op.eng}_{k}"
                gen = dma_gen.get(base, 0)
                c = dma_cnt.get(base, 0) + 16
                if c > self.max_sem:
                    gen += 1
                    dma_gen[base] = gen
                    c = 16
                dma_cnt[base] = c
                op.sem = f"{base}_{gen}"
                op.semval = c
                pv = prev_dma_on_sem.get(base)
                if pv is not None:
                    op.deps.add(pv)
                prev_dma_on_sem[base] = op.idx
        for op in ops:
            for d in op.deps:
                ops[d].needed = True
        for op in ops:
            if (not op.is_dma) and op.needed:
                if eng_cnt[op.eng] >= self.max_sem:
                    eng_sem_idx[op.eng] += 1
                    eng_cnt[op.eng] = 0
                eng_cnt[op.eng] += 1
                op.sem = f"e_{op.eng}_{eng_sem_idx[op.eng]}"
                op.semval = eng_cnt[op.eng]
        names = sorted({op.sem for op in ops if op.sem is not None})
        sems = {n: semctx(n) for n in names}
        per_eng = {e: [] for e in ALLQ}
        for op in ops:
            per_eng[op.eng].append(op)

        def run_engine(ename, eh):
            waited = {}
            for op in per_eng[ename]:
                need = {}
                for d in op.deps:
                    p = ops[d]
                    if p.semval is None:
                        continue
                    if need.get(p.sem, 0) < p.semval:
                        need[p.sem] = p.semval
                for sname, v in need.items():
                    if waited.get(sname, 0) >= v:
                        continue
                    eh.wait_ge(sems[sname], v)
                    waited[sname] = v
                ins = op.fn(eh)
                if op.sem is not None:
                    if op.cc:
                        ins.then_inc(sems[op.sem])
                    else:
                        ins.then_inc(sems[op.sem], 16 if op.is_dma else 1)

        last = {}
        for op in ops:
            if op.is_dma and not op.cc:
                last[op.sem] = max(last.get(op.sem, 0), op.semval)
        return run_engine, sems, last


def build_program(Tp, n_samp, Ts=32, L=DEPTH, NTP=512, seg=True, groups=((0, 1, 2, 3), (4, 5, 6, 7))):
    nc = bass.Bass("TRN2", target_bir_lowering=False)
    S = Sched()
    es = ExitStack()

    def din(name, shape):
        return nc.dram_tensor(name, list(shape), F32, kind="ExternalInput").ap()

    def dout(name, shape):
        return nc.dram_tensor(name, list(shape), F32, kind="ExternalOutput").ap()

    xp_in = din("xp_in", [Tp, D])
    pp_in = din("pp_in", [L, Tp, PLE])
    xs_in = din("xs_in", [n_samp, Ts, D])
    ps_in = din("ps_in", [L, n_samp, Ts, PLE])
    ca_in = din("ca_in", [L, n_samp, KA - 1, DA])
    cb_in = din("cb_in", [L, n_samp, KB - 1, DXBC])
    st_in = din("st_in", [L, n_samp, DI, NS])
    w_in = din("w_in", [L, D, DIN])
    w_a_out = din("w_a_out", [L, DA, D])
    w_b_out = din("w_b_out", [L, DI, D])
    w_out = din("w_out", [L, D, D])
    w_ple = din("w_ple", [L, PLE, D])
    b_main = din("b_main", [128, L, 88])
    b_dt = din("b_dt", [L, NH])
    dt_bias = din("dt_bias", [L, NH])
    a_log = din("a_log", [L, NH])
    caw = din("caw", [128, L, 8, KA])
    cab = din("cab", [128, L, 8])
    nag = din("nag", [128, L, 8])
    nab = din("nab", [128, L, 8])
    cbw = din("cbw", [128, L, 24, KB])
    cbb = din("cbb", [128, L, 24])
    gnw = din("gnw", [128, L, 16])
    lng = din("lng", [128, L, 8])
    lnb = din("lnb", [128, L, 8])
    dsk = din("dsk", [128, L, 16])
    c_ident = din("c_ident", [128, 128])
    c_ones = din("c_ones", [128, 128])
    c_U = din("c_U", [128, 128])
    c_Lw = din("c_Lw", [128, 128])
    xprev = din("xprev", [KA - 1, D])
    c_Ubd = din("c_Ubd", [128, 128])
    c_Lwbd = din("c_Lwbd", [128, 128])
    c_sqm = din("c_sqm", [128, 4])
    hflag_in = din("hflag_in", [128, 32])
    msel_in = din("msel_in", [128, 4])
    mH_in = din("mH_in", [128, 4])
    mHc_in = din("mHc_in", [128, 4])

    yp = dout("yp", [Tp, D])
    ys = dout("ys", [n_samp, Ts, D])
    cap_o = dout("cap_o", [L, KA - 1, DA])
    cbp_o = dout("cbp_o", [L, KB - 1, DXBC])
    stp_o = dout("stp_o", [L, DI, NS])
    cas_o = dout("cas_o", [L, n_samp, KA - 1, DA])
    cbs_o = dout("cbs_o", [L, n_samp, KB - 1, DXBC])
    sts_o = dout("sts_o", [L, n_samp, DI, NS])
    xsc_p = nc.dram_tensor("xsc_p", [128, 8, Tp], F32).ap()
    xsc_s = nc.dram_tensor("xsc_s", [128, 8, n_samp * Ts], F32).ap()
    G = len(groups[0])
    xcs = nc.dram_tensor("xcs", [max(1, Tp // NTP), 128, 24, NTP], BF16).ap()
    xh_src = nc.dram_tensor("xh_src", [128, 8 * (KA - 1)], F32).ap()
    xh_all = nc.dram_tensor("xh_all", [G * 128, 8 * (KA - 1)], F32).ap()
    st_src = nc.dram_tensor("st_src", [128, DI], F32).ap()
    st_all = nc.dram_tensor("st_all", [G * 128, DI], F32).ap()
    dt_src = nc.dram_tensor("dt_src", [128, 16], F32).ap()
    dt_all = nc.dram_tensor("dt_all", [G * 128, 16], F32).ap()

    def sb(name, shape, dt=F32):
        return es.enter_context(nc.sbuf_tensor(name, list(shape), dt))

    NT = NTP
    identf = sb("identf", [128, 128]); identb = sb("identb", [128, 128], BF16)
    onesf = sb("onesf", [128, 128]); onesb = sb("onesb", [128, 128], BF16)
    Uf = sb("Uf", [128, 128]); Lwf = sb("Lwf", [128, 128])
    Ubd = sb("Ubd", [128, 128]); Lwbd = sb("Lwbd", [128, 128]); sqm = sb("sqm", [128, 4])
    t_bmain = sb("t_bmain", [128, L, 88])
    t_dtb = sb("t_dtb", [128, L, NH]); t_dtb2 = sb("t_dtb2", [128, L, NH]); t_A = sb("t_A", [128, L, NH])
    t_caw = sb("t_caw", [128, L, 8, KA]); t_cab = sb("t_cab", [128, L, 8])
    t_nag = sb("t_nag", [128, L, 8]); t_nab = sb("t_nab", [128, L, 8])
    t_cbw = sb("t_cbw", [128, L, 24, KB]); t_cbb = sb("t_cbb", [128, L, 24])
    t_gnw = sb("t_gnw", [128, L, 16]); t_lng = sb("t_lng", [128, L, 8]); t_lnb = sb("t_lnb", [128, L, 8])
    t_dsk = sb("t_dsk", [128, L, 16])
    ddsk = sb("ddsk", [128, 16, 128], BF16)

    xb = sb("xb", [128, 8, NT], BF16)
    pT = sb("pT", [128, 2, NT], BF16)
    Ureg = sb("Ureg", [128, 8, NT])
    Uflat = Ureg[:].rearrange("p c t -> p (c t)")
    Zraw = sb("Zraw", [128, 8 * NT])
    Zreg = Zraw[:].bitcast(BF16).rearrange("p (c t) -> p c t", c=16)
    stg = Zraw[:].rearrange("p (a d) -> p a d", d=D)
    uext = sb("uext", [128, 8, KA - 1 + NT], BF16)
    gA = sb("gA", [128, 8, NT], BF16)
    m1 = sb("m1", [128, 8, NT], BF16)
    xc = sb("xc", [128, 24, NT], BF16)
    hB = sb("hB", [128, 24, 4 * (KB - 1)], BF16)
    NWB = 4
    wbuf = [sb(f"wbuf{i}", [128, 4096], BF16) for i in range(2)]
    wdt = sb("wdt", [128, 8, NH], BF16)
    dA = [sb(f"dA{i}", [128, KA, 128], BF16) for i in range(2)]
    dB = [sb(f"dB{i}", [128, KB, 128], BF16) for i in range(2)]
    xpb = [sb(f"xpb{i}", [128, KB - 1 + NT], BF16) for i in range(2)]
    rotf = [sb(f"rotf{i}", [128, NT]) for i in range(4)]
    rotb = [sb(f"rotb{i}", [128, NT], BF16) for i in range(2)]
    st_mean = sb("st_mean", [128, NT]); st_rstd = sb("st_rstd", [128, NT]); st_nmr = sb("st_nmr", [128, NT])
    st_tmp = st_nmr
    NBK = max(1, NT // 128)
    dtT = sb("dtT", [128, NBK, NH]); aT = sb("aT", [128, NBK, NH])
    dtv = dtT
    BT = sb("BT", [128, NG, 128], BF16)
    BTm = sb("BTm", [128, NG, 128], BF16)
    dec4 = sb("dec4", [128, 16, 4])
    htmp = sb("htmp", [128, 4 * (KA - 1)], BF16)
    xdx = sb("xdx", [128, 2 * DI], BF16)
    xdt = xdx[:, 0:DI]
    xdtd = xdx[:, DI:2 * DI]
    d2e = sb("d2e", [128, NH])
    abc = sb("abc", [128, DI])
    wbuf.append(abc[:].bitcast(BF16))
    wbuf.append(xdx[:])
    CBm = sb("CBm", [128, NG, 128], BF16)
    rhsA = [sb(f"rhsA{i}", [128, 8, 128]) for i in range(1)] * 2
    eseg = [sb(f"eseg{i}", [128, 8, 128], BF16) for i in range(2)]
    MT = eseg
    Et = [sb(f"Et{i}", [128, 4, 128]) for i in range(2)]
    t1 = Et
    Sn = sb("Sn", [128, 16, 128])
    Snb = xdtd.rearrange("p (c n) -> p c n", c=16)
    Sb_ = sb("Sb", [128, DI], BF16)
    dec = sb("dec", [128, 16])
    ptok = abc[:, :1024].rearrange("p (b d) -> p b d", d=PLE)
    ua = gA

    xh_b = sb("xh_b", [128, 8, 32], BF16)
    xhf = sb("xhf", [128, 8 * (KA - 1)])
    sgh = xhf[:, :128].rearrange("p (j t) -> p j t", j=4)
    hflag = sb("hflag", [128, 32])
    msel = sb("msel", [128, 4]); mH = sb("mH", [128, 4]); mHc = sb("mHc", [128, 4])
    dtot = sb("dtot", [128, 16]); dr = sb("dr", [128, 16]); ar = sb("ar", [128, 16])
    psb = [es.enter_context(nc.psum_tensor(f"psb{i}", [128, 512], F32)) for i in range(8)]

    def B(n):
        return Buf(n)

    b_const = B("const")
    b_par = B("par")
    b_ddsk = B("ddsk")
    b_xb = [B(f"xb{c}") for c in range(8)]
    b_pT = B("pT")
    b_U = [B(f"U{c}") for c in range(8)]
    b_Z = [B(f"Z{c}") for c in range(16)]
    b_uext = [B(f"uext{c}") for c in range(8)]
    b_gA = [B(f"gA{c}") for c in range(8)]
    b_m1 = [B(f"m1{c}") for c in range(8)]
    b_xc = [B(f"xc{c}") for c in range(24)]
    b_hB = [B(f"hB{c}") for c in range(24)]
    b_wbuf = [[B("wbuf0")], [B("wbuf1")]]
    b_wdt = B("wdt")
    b_dA = [B("dA0"), B("dA1")]; b_dB = [B("dB0"), B("dB1")]; b_xpb = [B("xpb0"), B("xpb1")]
    b_rotf = [B(f"rotf{i}") for i in range(4)]; b_rotb = [B(f"rotb{i}") for i in range(2)]
    b_mean = B("mean"); b_rstd = B("rstd"); b_nmr = B("nmr"); b_sttmp = b_nmr
    b_dtT = B("dtT"); b_aT = B("aT"); b_dtv = b_dtT
    b_BTm = B("BTm"); b_dec4 = B("dec4"); b_htmp = B("htmp")
    b_BT = B("BT"); b_xdt = B("xdt"); b_xdtd = B("xdtd"); b_d2e = B("d2e"); b_abc = B("abc"); b_CBm = B("CBm")
    b_wbuf.append([b_abc]); b_wbuf.append([b_xdt, b_xdtd])
    b_rhsA = [B("rhsA0")] * 2; b_eseg = [B("eseg0"), B("eseg1")]; b_MT = b_eseg
    b_Et = [B("Et0"), B("Et1")]; b_t1 = b_Et
    b_Sn = B("Sn"); b_Snb = b_xdtd; b_Sb = B("Sb"); b_dec = B("dec")
    b_ua = b_gA
    XB = [xb, gA]
    b_XB = [b_xb, b_gA]
    PAR = [0]
    PREF = [False]
    b_ps = [Buf(f"ps{i}", excl=True) for i in range(8)]
    b_out = B("out")
    b_xh = B("xh"); b_xhf = B("xhf"); b_sgh = b_xhf; b_msk = B("msk"); b_dtot = B("dtot"); b_dr = B("dr"); b_ar = B("ar")
    b_xhsrc = B("xhsrc"); b_xhall = B("xhall"); b_stsrc = B("stsrc"); b_stall = B("stall"); b_dtsrc = B("dtsrc"); b_dtall = B("dtall")
    b_xsc = {}

    def xsc_buf(key):
        if key not in b_xsc:
            b_xsc[key] = B(f"xsc{key}")
        return b_xsc[key]

    rr = {"ps": 0, "rf": 0, "rb": 0, "w": 0, "dA": 0, "dB": 0, "xp": 0, "g": 0, "e": 0}

    def nxt(key, n):
        i = rr[key] % n
        rr[key] += 1
        return i

    PBN = [6]
    WBN = [NWB]

    def pbank():
        return nxt("ps", PBN[0])

    def add(eng, fn, reads=(), writes=(), dma=False, cc=False):
        return S.add(eng, fn, reads, writes, dma, cc)

    add("sp", lambda e: e.dma_start(out=identf[:], in_=c_ident[:, :]), writes=[b_const], dma=True)
    add("sp", lambda e: e.dma_start(out=onesf[:], in_=c_ones[:, :]), writes=[b_const], dma=True)
    add("sp", lambda e: e.dma_start(out=Uf[:], in_=c_U[:, :]), writes=[b_const], dma=True)
    add("sp", lambda e: e.dma_start(out=Lwf[:], in_=c_Lw[:, :]), writes=[b_const], dma=True)
    add("sp", lambda e: e.dma_start(out=Ubd[:], in_=c_Ubd[:, :]), writes=[b_const], dma=True)
    add("sp", lambda e: e.dma_start(out=Lwbd[:], in_=c_Lwbd[:, :]), writes=[b_const], dma=True)
    add("sp", lambda e: e.dma_start(out=sqm[:], in_=c_sqm[:, :]), writes=[b_const], dma=True)
    for (t, d_) in ((hflag, hflag_in), (msel, msel_in), (mH, mH_in), (mHc, mHc_in)):
        add("sp", (lambda t, d_: lambda e: e.dma_start(out=t[:], in_=d_))(t, d_), writes=[b_msk], dma=True)
    add("dve", lambda e: e.tensor_copy(out=identb[:], in_=identf[:]), reads=[b_const], writes=[b_const])
    add("dve", lambda e: e.tensor_copy(out=onesb[:], in_=onesf[:]), reads=[b_const], writes=[b_const])
    for (t, d_) in ((t_bmain, b_main), (t_caw, caw), (t_cab, cab), (t_nag, nag), (t_nab, nab), (t_cbw, cbw),
                    (t_cbb, cbb), (t_gnw, gnw), (t_lng, lng), (t_lnb, lnb), (t_dsk, dsk)):
        add("sp", (lambda t, d_: lambda e: e.dma_start(out=t[:], in_=d_))(t, d_), writes=[b_par], dma=True)
    add("sp", lambda e: e.dma_start(out=t_dtb[:].rearrange("p l h -> p (l h)"),
                                    in_=b_dt.rearrange("l h -> (l h)").partition_broadcast(128)), writes=[b_par], dma=True)
    add("sp", lambda e: e.dma_start(out=t_dtb2[:].rearrange("p l h -> p (l h)"),
                                    in_=dt_bias.rearrange("l h -> (l h)").partition_broadcast(128)), writes=[b_par], dma=True)
    add("sp", lambda e: e.dma_start(out=t_A[:].rearrange("p l h -> p (l h)"),
                                    in_=a_log.rearrange("l h -> (l h)").partition_broadcast(128)), writes=[b_par], dma=True)
    add("dve", lambda e: e.tensor_tensor(out=t_dtb[:], in0=t_dtb[:], in1=t_dtb2[:], op=ALU.add), reads=[b_par], writes=[b_par])
    add("act", lambda e: e.activation(out=t_A[:], in_=t_A[:], func=AF.Exp), reads=[b_par], writes=[b_par])
    add("dve", lambda e: e.tensor_scalar_mul(out=t_A[:], in0=t_A[:], scalar1=-1.0), reads=[b_par], writes=[b_par])

    def mm(out, lhsT, rhs, start, stop, reads, writes):
        add("pe", lambda e: e.matmul(out, lhsT=lhsT, rhs=rhs, start=start, stop=stop), reads=reads, writes=writes)

    def tr(out, in_, ident, reads, writes):
        add("pe", lambda e: e.transpose(out, in_, ident), reads=reads, writes=writes)

    def ln_stats(pi_sum, pi_sq, n, nt):
        inv = 1.0 / n
        add("act", lambda e: e.activation(out=st_mean[:, :nt], in_=psb[pi_sum][:, :nt], func=AF.Identity, scale=inv),
            reads=[b_ps[pi_sum]], writes=[b_mean])
        add("dve", lambda e: e.tensor_tensor(out=st_tmp[:, :nt], in0=st_mean[:, :nt], in1=st_mean[:, :nt], op=ALU.mult),
            reads=[b_mean], writes=[b_sttmp])
        add("dve", lambda e: e.scalar_tensor_tensor(out=st_tmp[:, :nt], in0=psb[pi_sq][:, :nt], scalar=inv, in1=st_tmp[:, :nt],
                                                    op0=ALU.mult, op1=ALU.subtract),
            reads=[b_ps[pi_sq], b_sttmp], writes=[b_sttmp])
        add("act", lambda e: e.activation(out=st_rstd[:, :nt], in_=st_tmp[:, :nt], func=AF.Ln, bias=EPS, scale=1.0),
            reads=[b_sttmp], writes=[b_rstd])
        add("act", lambda e: e.activation(out=st_rstd[:, :nt], in_=st_rstd[:, :nt], func=AF.Exp, scale=-0.5),
            reads=[b_rstd], writes=[b_rstd])
        add("dve", lambda e: e.scalar_tensor_tensor(out=st_nmr[:, :nt], in0=st_mean[:, :nt], scalar=-1.0, in1=st_rstd[:, :nt],
                                                    op0=ALU.mult, op1=ALU.mult),
            reads=[b_mean, b_rstd], writes=[b_nmr])

    def tile_layer(l, nt, qb, x_tok_ap, p_tok_ap, xsc_ap, xsc_key, y_tok_ap, first, last,
                   ca_in_ap, cb_in_ap, st_in_ap, ca_out_ap, cb_out_ap, st_out_ap, mode="full", xcs_ap=None, seg=False, nsq=1, next_x=None):
        nb = nt // qb
        bx = xsc_buf(xsc_key)
        do_a = mode in ("p2", "full")
        do_xbc = mode in ("p1", "full")
        state_only = mode == "p1"
        xb = XB[PAR[0]]; gA = XB[1 - PAR[0]]; ua = gA
        b_xb = b_XB[PAR[0]]; b_gA = b_XB[1 - PAR[0]]; b_ua = b_gA
        tl = nt // nsq
        WA = KA - 1 + tl
        WB = KB - 1 + tl
        Um = Ubd if nsq > 1 else Uf
        Lm = Lwbd if nsq > 1 else Lwf

        def sv(ap2d, w=None):
            return ap2d.rearrange("p (s w) -> p s w", s=nsq)
        stage(0)

        if l == 0 and mode != "p2":
            add("sp", lambda e: e.dma_start(out=stg[:qb, :nb, :], in_=x_tok_ap.rearrange("(b p) d -> p b d", p=qb)),
                writes=b_Z, dma=True)
            for c in range(8):
                pi = pbank()
                for b in range(nb):
                    tr(psb[pi][:, b * qb:(b + 1) * qb], stg[:qb, b, c * 128:(c + 1) * 128], identf[:qb, :qb],
                       reads=b_Z + [b_const], writes=[b_ps[pi]])
                add("act", lambda e, c=c, pi=pi: e.activation(out=xb[:, c, :nt], in_=psb[pi][:, :nt], func=AF.Copy),
                    reads=[b_ps[pi]], writes=[b_xb[c]])
                add("dve", lambda e, c=c, pi=pi: e.tensor_copy(out=Ureg[:, c, :nt], in_=psb[pi][:, :nt]),
                    reads=[b_ps[pi]], writes=[b_U[c]])
            stage(0.3)
            add("sp", lambda e: e.dma_start(out=xsc_ap, in_=Ureg[:, :, :nt]), reads=b_U, writes=[bx], dma=True)
        elif PREF[0]:
            PREF[0] = False
        else:
            add("pool", lambda e: e.dma_start(out=xb[:, :, :nt], in_=xsc_ap), reads=[bx], writes=b_xb, dma=True)
        stage(0.6)
        stage(1)
        def proj_group(col0, nchunks, consume, extra=None):
            for _ in proj_group_g(col0, nchunks, consume, extra):
                pass

        def proj_group_g(col0, nchunks, consume, extra=None):
            for g0 in range(0, nchunks, 4):
                ng = min(4, nchunks - g0)
                wi = nxt("w", WBN[0])
                src = w_in[l, :, col0 + g0 * 128: col0 + (g0 + ng) * 128].rearrange("(k p) n -> p k n", p=128)
                view = wbuf[wi][:, :8 * ng * 128].rearrange("p (k n) -> p k n", k=8)
                add("pool", lambda e, view=view, src=src: e.dma_start(out=view, in_=src), writes=b_wbuf[wi], dma=True)
                for j in range(ng):
                    pi = pbank()
                    for k in range(8):
                        mm(psb[pi][:, :nt], view[:, k, j * 128:(j + 1) * 128], xb[:, k, :nt], k == 0, k == 7,
                           reads=b_wbuf[wi] + [b_xb[k]], writes=[b_ps[pi]])
                    if extra is None:
                        consume(g0 + j, pi)
                        yield
                    else:
                        ne = extra.shape[2]
                        pih = pbank()
                        for k in range(8):
                            mm(psb[pih][:, :ne], view[:, k, j * 128:(j + 1) * 128], extra[:, k, :], k == 0, k == 7,
                               reads=b_wbuf[wi] + [b_xh], writes=[b_ps[pih]])
                        consume(g0 + j, pi, pih)
                        yield

        def wmat(src2d, kch, ncols):
            wi = nxt("w", WBN[0])
            src = src2d.rearrange("(k p) n -> p k n", p=128)
            view = wbuf[wi][:, :kch * ncols].rearrange("p (k n) -> p k n", k=kch)
            add("pool", lambda e: e.dma_start(out=view, in_=src), writes=b_wbuf[wi], dma=True)
            return wi, view

        def gen_branch_a():
            halo_a = first and seg
            if first and not seg:
                if ca_in_ap is None:
                    add("dve", lambda e: e.memset(uext[:, :, 0:KA - 1], 0.0), writes=b_uext)
                else:
                    nh = nsq * (KA - 1)
                    add("sp", lambda e: e.dma_start(out=stg[:nh, 0, :], in_=ca_in_ap), writes=b_Z, dma=True)
                    for c0 in (0, 4):
                        pi = pbank()
                        for cc in range(4):
                            c = c0 + cc
                            tr(psb[pi][:, cc * 128:cc * 128 + nh], stg[:nh, 0, c * 128:(c + 1) * 128], identf[:nh, :nh],
                               reads=b_Z + [b_const], writes=[b_ps[pi]])
                        for cc in range(4):
                            c = c0 + cc
                            add("act", lambda e, pi=pi, c=c, cc=cc: e.activation(out=sv(uext[:, c, :nsq * WA])[:, :, 0:KA - 1],
                                                                                 in_=sv(psb[pi][:, cc * 128:cc * 128 + nh]), func=AF.Copy),
                                reads=[b_ps[pi]], writes=[b_uext[c]])
            val_ps = {}

            def cons_val(j, pi):
                val_ps[j] = pi

            for g0 in (0, 4):
                sg_i = {}

                def cons_glu(j, pi, pih=None, g0=g0):
                    ri = nxt("rf", 4)
                    sg_i[j] = ri
                    if pih is not None:
                        add("act", lambda e, j=j, pih=pih: e.activation(out=sgh[:, j, :KA - 1], in_=psb[pih][:, :KA - 1], func=AF.Sigmoid,
                                                                        bias=t_bmain[:, l, BI_GLU + g0 + j:BI_GLU + g0 + j + 1]),
                            reads=[b_ps[pih], b_par], writes=[b_sgh])
                    add("act", lambda e, j=j, pi=pi, ri=ri: e.activation(out=rotf[ri][:, :nt], in_=psb[pi][:, :nt], func=AF.Sigmoid,
                                                                           bias=t_bmain[:, l, BI_GLU + g0 + j:BI_GLU + g0 + j + 1]),
                        reads=[b_ps[pi], b_par], writes=[b_rotf[ri]])

                yield from proj_group_g(C_GLU + g0 * 128, 4, cons_glu, extra=(xh_b[:, :, 0:KA - 1] if halo_a else None))

                def cons_val2(j, pi, pih=None, g0=g0):
                    ri = sg_i[j]
                    c = g0 + j
                    if pih is not None:
                        add("dve", lambda e, c=c, j=j, pih=pih: e.scalar_tensor_tensor(
                            out=uext[:, c, 0:KA - 1], in0=psb[pih][:, :KA - 1], scalar=t_bmain[:, l, BI_VAL + c:BI_VAL + c + 1],
                            in1=sgh[:, j, :KA - 1], op0=ALU.add, op1=ALU.mult),
                            reads=[b_ps[pih], b_par, b_sgh], writes=[b_uext[c]])
                        add("dve", lambda e, c=c: e.tensor_scalar_mul(out=uext[:, c, 0:KA - 1], in0=uext[:, c, 0:KA - 1], scalar1=hflag[:, 0:1]),
                            reads=[b_uext[c], b_msk], writes=[b_uext[c]])
                    add("dve", lambda e, c=c, pi=pi, ri=ri: e.scalar_tensor_tensor(
                        out=sv(uext[:, c, :nsq * WA])[:, :, KA - 1:WA], in0=sv(psb[pi][:, :nt]), scalar=t_bmain[:, l, BI_VAL + c:BI_VAL + c + 1],
                        in1=sv(rotf[ri][:, :nt]), op0=ALU.add, op1=ALU.mult),
                        reads=[b_ps[pi], b_par, b_rotf[ri]], writes=[b_uext[c]])

                yield from proj_group_g(C_VAL + g0 * 128, 4, cons_val2, extra=(xh_b[:, :, 0:KA - 1] if halo_a else None))

            stage(2)
            pend_st = []
            for c in range(8):
                di = nxt("dA", 2)
                def build_dA(e, di=di, c=c):
                    ins = None
                    for k in range(KA):
                        ins = e.tensor_scalar_mul(out=dA[di][:, k, :], in0=identb[:], scalar1=t_caw[:, l, c, k:k + 1])
                    return ins

                add("dve", build_dA, reads=[b_const, b_par], writes=[b_dA[di]])
                pi = 5 if interleave else pbank()
                for k in range(KA):
                    mm(sv(psb[pi][:, :nt]), dA[di][:, k, :], sv(uext[:, c, :nsq * WA])[:, :, k:k + tl], k == 0, k == KA - 1,
                       reads=[b_dA[di], b_uext[c]], writes=[b_ps[pi]])
                    if interleave and k % 8 == 7:
                        yield
                if pend_st:
                    pend_st.pop()()
                ri = nxt("rf", 4)
                add("act", lambda e, c=c, pi=pi: e.activation(out=Ureg[:, c, :nt], in_=psb[pi][:, :nt], func=AF.Identity,
                                                              bias=t_cab[:, l, c:c + 1]),
                    reads=[b_ps[pi], b_par], writes=[b_U[c]])
                add("act", lambda e, c=c, pi=pi, ri=ri: e.activation(out=rotf[ri][:, :nt], in_=psb[pi][:, :nt], func=AF.Square,
                                                                       bias=t_cab[:, l, c:c + 1]),
                    reads=[b_ps[pi], b_par], writes=[b_rotf[ri]])
                def stats_a(c=c, ri=ri):
                    mm(psb[6][:, :nt], onesf[:], Ureg[:, c, :nt], c == 0, c == 7, reads=[b_const, b_U[c]], writes=[b_ps[6]])
                    mm(psb[7][:, :nt], onesf[:], rotf[ri][:, :nt], c == 0, c == 7, reads=[b_const, b_rotf[ri]], writes=[b_ps[7]])

                pend_st.append(stats_a)
                yield
            while pend_st:
                pend_st.pop()()
            if last and ca_out_ap is not None:
                pi = pbank()
                pv = psb[pi][:].bitcast(BF16)
                nh = nsq * (KA - 1)
                for c in range(8):
                    add("dve", lambda e, c=c: e.tensor_copy(out=sv(htmp[:, :nh]), in_=sv(uext[:, c, :nsq * WA])[:, :, tl:tl + KA - 1]),
                        reads=[b_uext[c]], writes=[b_htmp])
                    tr(pv[:nh, c * 128:(c + 1) * 128], htmp[:, :nh], identb[:],
                       reads=[b_htmp, b_const], writes=[b_ps[pi]])
                for hf in range(2):
                    ri = nxt("rf", 4)
                    add("act", lambda e, pv=pv, ri=ri, hf=hf: e.activation(out=rotf[ri][:nh, :512], in_=pv[:nh, hf * 512:(hf + 1) * 512], func=AF.Copy),
                        reads=[b_ps[pi]], writes=[b_rotf[ri]])
                    add("sp", lambda e, ri=ri, hf=hf: e.dma_start(out=ca_out_ap[:, hf * 512:(hf + 1) * 512], in_=rotf[ri][:nh, :512]),
                        reads=[b_rotf[ri]], writes=[b_out], dma=True)
            if not last:
                add("dve", lambda e: e.tensor_copy(out=uext[:, :, 0:KA - 1], in_=uext[:, :, nt:nt + KA - 1]),
                    reads=b_uext, writes=b_uext)
            ln_stats(6, 7, DA, nt)
            yield

            stage(3)
            def cons_gate(j, pi):
                add("act", lambda e, j=j, pi=pi: e.activation(out=gA[:, j, :nt], in_=psb[pi][:, :nt], func=AF.Silu,
                                                              bias=t_bmain[:, l, BI_GATE + j:BI_GATE + j + 1]),
                    reads=[b_ps[pi], b_par], writes=[b_gA[j]])

            yield from proj_group_g(C_GATE, 8, cons_gate)
            for c in range(8):
                ri = nxt("rf", 4)
                rb = nxt("rb", 2)
                add("dve", lambda e, c=c, ri=ri: e.tensor_tensor(out=rotf[ri][:, :nt], in0=Ureg[:, c, :nt], in1=st_rstd[:, :nt], op=ALU.mult),
                    reads=[b_U[c], b_rstd], writes=[b_rotf[ri]])
                add("dve", lambda e, ri=ri: e.tensor_tensor(out=rotf[ri][:, :nt], in0=rotf[ri][:, :nt], in1=st_nmr[:, :nt], op=ALU.add),
                    reads=[b_rotf[ri], b_nmr], writes=[b_rotf[ri]])
                add("act", lambda e, c=c, ri=ri, rb=rb: e.activation(out=rotb[rb][:, :nt], in_=rotf[ri][:, :nt], func=AF.Silu,
                                                                       bias=t_nab[:, l, c:c + 1], scale=t_nag[:, l, c:c + 1]),
                    reads=[b_rotf[ri], b_par], writes=[b_rotb[rb]])
                add("dve", lambda e, c=c, rb=rb: e.tensor_tensor(out=ua[:, c, :nt], in0=rotb[rb][:, :nt], in1=gA[:, c, :nt], op=ALU.mult),
                    reads=[b_rotb[rb], b_gA[c]], writes=[b_ua[c]])
                yield
            for g0 in (0, 4):
                sg_i = {}

                def cons_ga(j, pi, g0=g0):
                    ri = nxt("rf", 4)
                    sg_i[j] = ri
                    add("act", lambda e, j=j, pi=pi, ri=ri: e.activation(out=rotf[ri][:, :nt], in_=psb[pi][:, :nt], func=AF.Sigmoid,
                                                                           bias=t_bmain[:, l, BI_GA + g0 + j:BI_GA + g0 + j + 1]),
                        reads=[b_ps[pi], b_par], writes=[b_rotf[ri]])

                yield from proj_group_g(C_GA + g0 * 128, 4, cons_ga)
                wi, wv = wmat(w_a_out[l, :, g0 * 128:(g0 + 4) * 128], 8, 512)
                for j in range(4):
                    oc = g0 + j
                    pi = pbank()
                    for k in range(8):
                        mm(psb[pi][:, :nt], wv[:, k, j * 128:(j + 1) * 128], ua[:, k, :nt], k == 0, k == 7,
                           reads=b_wbuf[wi] + [b_ua[k]], writes=[b_ps[pi]])
                    ri = sg_i[j]
                    add("dve", lambda e, oc=oc, pi=pi, ri=ri: e.tensor_tensor(out=m1[:, oc, :nt], in0=psb[pi][:, :nt], in1=rotf[ri][:, :nt], op=ALU.mult),
                        reads=[b_ps[pi], b_rotf[ri]], writes=[b_m1[oc]])
                    yield

        interleave = (mode == "p2" and nsq == 1 and INTERLEAVE)
        ga = gen_branch_a() if do_a else iter(())

        def pump(n=1):
            for _ in range(n):
                try:
                    next(ga)
                except StopIteration:
                    return

        def drain():
            for _ in ga:
                pass

        if not interleave:
            drain()
        stage(4)
        if do_a:
            def cons_z(j, pi):
                add("act", lambda e, j=j, pi=pi: e.activation(out=Zreg[:, j, :nt], in_=psb[pi][:, :nt], func=AF.Silu,
                                                              bias=t_bmain[:, l, BI_Z + j:BI_Z + j + 1]),
                    reads=[b_ps[pi], b_par], writes=[b_Z[j]])

            proj_group(C_Z, 16, cons_z)
        else:
            pass

        if do_xbc:
            halo_b = first and seg
            if first and not seg:
                if cb_in_ap is None:
                    add("dve", lambda e: e.memset(hB[:, :, :KB - 1], 0.0), writes=b_hB)
                else:
                    nhb = nsq * (KB - 1)
                    add("sp", lambda e: e.dma_start(out=Uflat[:nhb, :DXBC], in_=cb_in_ap), writes=b_U, dma=True)
                    pi = pbank()
                    for c in range(24):
                        tr(psb[pi][:, c * 16:c * 16 + nhb], Uflat[:nhb, c * 128:(c + 1) * 128], identf[:nhb, :nhb],
                           reads=b_U + [b_const], writes=[b_ps[pi]])
                    add("act", lambda e, pi=pi: e.activation(out=hB[:, :, :nhb], in_=psb[pi][:, :384].rearrange("p (c t) -> p c t", c=24)[:, :, 0:nhb],
                                                             func=AF.Copy),
                        reads=[b_ps[pi]], writes=b_hB)

            pend_xbc = []

            def cons_xbc(j, pi, pih=None):
                if pend_xbc:
                    pend_xbc.pop()()
                xi = nxt("xp", 2)
                di = nxt("dB", 2)
                if pih is not None:
                    add("dve", lambda e, j=j, pih=pih: e.scalar_tensor_tensor(
                        out=hB[:, j, :KB - 1], in0=psb[pih][:, :KB - 1], scalar=t_bmain[:, l, BI_XBC + j:BI_XBC + j + 1],
                        in1=hflag[:, 0:KB - 1], op0=ALU.add, op1=ALU.mult),
                        reads=[b_ps[pih], b_par, b_msk], writes=[b_hB[j]])
                add("dve", lambda e, xi=xi, j=j: e.tensor_copy(out=sv(xpb[xi][:, :nsq * WB])[:, :, 0:KB - 1], in_=sv(hB[:, j, :nsq * (KB - 1)])),
                    reads=[b_hB[j]], writes=[b_xpb[xi]])
                add("act", lambda e, xi=xi, j=j, pi=pi: e.activation(out=sv(xpb[xi][:, :nsq * WB])[:, :, KB - 1:WB], in_=sv(psb[pi][:, :nt]), func=AF.Identity,
                                                                       bias=t_bmain[:, l, BI_XBC + j:BI_XBC + j + 1]),
                    reads=[b_ps[pi], b_par], writes=[b_xpb[xi]])
                def build_dB(e, di=di, j=j):
                    ins = None
                    for k in range(KB):
                        ins = e.tensor_scalar_mul(out=dB[di][:, k, :], in0=identb[:], scalar1=t_cbw[:, l, j, k:k + 1])
                    return ins

                add("dve", build_dB, reads=[b_const, b_par], writes=[b_dB[di]])

                def part2(j=j, xi=xi, di=di):
                    p2 = pbank()
                    for k in range(KB):
                        mm(sv(psb[p2][:, :nt]), dB[di][:, k, :], sv(xpb[xi][:, :nsq * WB])[:, :, k:k + tl], k == 0, k == KB - 1,
                           reads=[b_dB[di], b_xpb[xi]], writes=[b_ps[p2]])
                    add("act", lambda e, j=j, p2=p2: e.activation(out=xc[:, j, :nt], in_=psb[p2][:, :nt], func=AF.Silu,
                                                                  bias=t_cbb[:, l, j:j + 1]),
                        reads=[b_ps[p2], b_par], writes=[b_xc[j]])
                    add("dve", lambda e, xi=xi, j=j: e.tensor_copy(out=sv(hB[:, j, :nsq * (KB - 1)]), in_=sv(xpb[xi][:, :nsq * WB])[:, :, tl:tl + KB - 1]),
                        reads=[b_xpb[xi]], writes=[b_hB[j]])

                pend_xbc.append(part2)

            proj_group(C_XBC, 24, cons_xbc, extra=(xh_b[:, :, KA - KB:KA - 1] if halo_b else None))
            while pend_xbc:
                pend_xbc.pop()()
            if xcs_ap is not None:
                add("sp", lambda e: e.dma_start(out=xcs_ap, in_=xc[:, :, :nt]), reads=b_xc, writes=[xsc_buf(("xcs",) + tuple(xsc_key))], dma=True)
            if last and cb_out_ap is not None:
                for h0 in range(0, 24, 8):
                    pi = pbank()
                    pv = psb[pi][:].bitcast(BF16)
                    nhb = nsq * (KB - 1)
                    for c in range(8):
                        tr(pv[:nhb, c * 128:(c + 1) * 128], hB[:, h0 + c, :nhb], identb[:],
                           reads=[b_hB[h0 + c], b_const], writes=[b_ps[pi]])
                    add("act", lambda e, pv=pv, h0=h0, nhb=nhb: e.activation(out=Uflat[:nhb, h0 * 128:(h0 + 8) * 128], in_=pv[:nhb, :1024], func=AF.Copy),
                        reads=[b_ps[pi]], writes=b_U)
                add("sp", lambda e: e.dma_start(out=cb_out_ap, in_=Uflat[:nsq * (KB - 1), :DXBC]), reads=b_U, writes=[b_out], dma=True)

        else:
            add("sp", lambda e: e.dma_start(out=xc[:, :, :nt], in_=xcs_ap), reads=[xsc_buf(("xcs",) + tuple(xsc_key))], writes=b_xc, dma=True)
        stage(5)
        add("pool", lambda e: e.dma_start(out=wdt[:], in_=w_in[l, :, C_DT:C_DT + NH].rearrange("(k p) n -> p k n", p=128)),
            writes=[b_wdt], dma=True)
        pi = pbank()
        for b in range(nb):
            for k in range(8):
                mm(psb[pi][:qb, b * NH:(b + 1) * NH], xb[:, k, b * qb:(b + 1) * qb], wdt[:, k, :], k == 0, k == 7,
                   reads=[b_xb[k], b_wdt], writes=[b_ps[pi]])
        add("dve", lambda e, pi=pi: e.tensor_tensor(out=dtv[:qb, :nb, :], in0=psb[pi][:qb, :nb * NH].rearrange("p (b h) -> p b h", b=nb),
                                                    in1=t_dtb[:qb, l:l + 1, :].to_broadcast([qb, nb, NH]), op=ALU.add),
            reads=[b_ps[pi], b_par], writes=[b_dtv])
        add("act", lambda e: e.activation(out=dtv[:qb, :nb, :], in_=dtv[:qb, :nb, :], func=AF.Exp), reads=[b_dtv], writes=[b_dtv])
        add("act", lambda e: e.activation(out=dtT[:qb, :nb, :], in_=dtv[:qb, :nb, :], func=AF.Ln, bias=1.0, scale=1.0),
            reads=[b_dtv], writes=[b_dtT])
        add("dve", lambda e: e.tensor_tensor(out=aT[:qb, :nb, :], in0=dtT[:qb, :nb, :],
                                             in1=t_A[:qb, l:l + 1, :].to_broadcast([qb, nb, NH]), op=ALU.mult),
            reads=[b_dtT, b_par], writes=[b_aT])

        stage(6)
        if first and do_a:
            def build_ddsk(e):
                ins = None
                for c in range(16):
                    ins = e.tensor_scalar_mul(out=ddsk[:, c, :], in0=identb[:], scalar1=t_dsk[:, l, c:c + 1])
                return ins

            add("dve", build_ddsk, reads=[b_const, b_par], writes=[b_ddsk])
        if first:
            if mode == "p1":
                add("dve", lambda e: e.memset(Sn[:], 0.0), writes=[b_Sn])
                add("dve", lambda e: e.memset(dtot[:], 1.0), writes=[b_dtot])
            elif mode == "full":
                if st_in_ap is None:
                    add("dve", lambda e: e.memset(Sn[:], 0.0), writes=[b_Sn])
                elif nsq == 1:
                    add("sp", lambda e: e.dma_start(out=Sn[:], in_=st_in_ap.rearrange("(c p) n -> p c n", p=128)), writes=[b_Sn], dma=True)

        if interleave:
            WBN[0] = 2
            PBN[0] = 5
        for b in range(nb):
            sl = slice(b * qb, (b + 1) * qb)
            if interleave:
                pump()
            pi = pbank()
            mm(psb[pi][:qb, :NH], Lm[:qb, :qb], aT[:qb, b, :], True, True, reads=[b_const, b_aT], writes=[b_ps[pi]])
            add("act", lambda e, pi=pi: e.activation(out=d2e[:qb, :], in_=psb[pi][:qb, :NH], func=AF.Exp), reads=[b_ps[pi]], writes=[b_d2e])
            add("dve", lambda e, b=b: e.tensor_copy(out=abc[:qb, :].rearrange("p (h q) -> p h q", q=HP),
                                                     in_=aT[:qb, b, :].unsqueeze(2).to_broadcast([qb, NH, HP])),
                reads=[b_aT], writes=[b_abc])
            if not state_only and nsq == 1:
                add("act", lambda e: e.activation(out=Snb, in_=Sn[:], func=AF.Copy), reads=[b_Sn], writes=[b_Snb])
                for half in range(2):
                    pi = pbank()
                    pv = psb[pi][:].bitcast(BF16)
                    for cc in range(8):
                        c = half * 8 + cc
                        tr(pv[:, cc * 128:(cc + 1) * 128], Snb[:, c, :], identb[:], reads=[b_Snb, b_const], writes=[b_ps[pi]])
                    add("dve", lambda e, pv=pv, half=half: e.tensor_copy(out=Sb_[:, half * 1024:(half + 1) * 1024], in_=pv[:, :1024]),
                        reads=[b_ps[pi]], writes=[b_Sb])
            if interleave:
                pump()
            pi = pbank()
            pv = psb[pi][:].bitcast(BF16)
            for g in range(NG):
                tr(pv[:qb, g * 128:(g + 1) * 128], xc[:, 16 + g, sl], identb[:], reads=[b_xc[16 + g], b_const], writes=[b_ps[pi]])
            add("act", lambda e, pv=pv: e.activation(out=BT[:qb, :, :], in_=pv[:qb, :512].rearrange("p (g n) -> p g n", g=NG), func=AF.Copy),
                reads=[b_ps[pi]], writes=[b_BT])
            for half in range(2):
                pi = pbank()
                pv = psb[pi][:].bitcast(BF16)
                for cc in range(8):
                    c = half * 8 + cc
                    tr(pv[:qb, cc * 128:(cc + 1) * 128], xc[:, c, sl], identb[:], reads=[b_xc[c], b_const], writes=[b_ps[pi]])
                add("dve", lambda e, pv=pv, half=half, b=b: e.tensor_tensor(
                    out=xdt[:qb, half * 1024:(half + 1) * 1024].rearrange("p (h q) -> p h q", q=HP),
                    in0=pv[:qb, :1024].rearrange("p (h q) -> p h q", q=HP),
                    in1=dtT[:qb, b, half * 16:(half + 1) * 16].unsqueeze(2).to_broadcast([qb, 16, HP]), op=ALU.mult),
                    reads=[b_ps[pi], b_dtT], writes=[b_xdt])
            if interleave:
                pump()
            add("dve", lambda e: e.tensor_tensor(out=xdtd[:qb, :].rearrange("p (h q) -> p h q", q=HP),
                                                  in0=xdt[:qb, :].rearrange("p (h q) -> p h q", q=HP),
                                                  in1=d2e[:qb, :].unsqueeze(2).to_broadcast([qb, NH, HP]), op=ALU.mult),
                reads=[b_xdt, b_d2e], writes=[b_xdtd])
            if nsq > 1:
                PBN[0] = 4
                pi = pbank()
                for c in range(16):
                    mm(psb[pi][:, c * 4:c * 4 + nsq], abc[:qb, c * 128:(c + 1) * 128], sqm[:qb, 0:nsq], True, True,
                       reads=[b_abc, b_const], writes=[b_ps[pi]])
                add("act", lambda e, pi=pi: e.activation(out=dec4[:, :, :nsq], in_=psb[pi][:, :64].rearrange("p (c s) -> p c s", s=4)[:, :, :nsq], func=AF.Exp),
                    reads=[b_ps[pi]], writes=[b_dec4])
                for i in range(nsq):
                    cs = slice(i * tl, (i + 1) * tl)
                    add("sp", lambda e, i=i: e.dma_start(out=Sn[:], in_=st_in_ap[i].rearrange("(c p) n -> p c n", p=128)), writes=[b_Sn], dma=True)
                    Snb2 = rhsA[0][:].rearrange("p h n -> p (h n)").bitcast(BF16).rearrange("p (c n) -> p c n", c=16)
                    add("act", lambda e, Snb2=Snb2: e.activation(out=Snb2, in_=Sn[:], func=AF.Copy), reads=[b_Sn], writes=[b_rhsA[0]])
                    for half in range(2):
                        pi = pbank()
                        pv = psb[pi][:].bitcast(BF16)
                        for cc in range(8):
                            c = half * 8 + cc
                            tr(pv[:, cc * 128:(cc + 1) * 128], Snb2[:, c, :], identb[:], reads=[b_rhsA[0], b_const], writes=[b_ps[pi]])
                        add("dve", lambda e, pv=pv, half=half: e.tensor_copy(out=Sb_[:, half * 1024:(half + 1) * 1024], in_=pv[:, :1024]),
                            reads=[b_ps[pi]], writes=[b_Sb])
                    for g in range(NG):
                        for jj in range(4):
                            c = 4 * g + jj
                            mm(psb[4 + g][:, jj * 128 + i * tl:jj * 128 + (i + 1) * tl], Sb_[:, c * 128:(c + 1) * 128], xc[:, 20 + g, cs], True, True,
                               reads=[b_Sb, b_xc[20 + g]], writes=[b_ps[4 + g]])
                    add("dve", lambda e, i=i: e.tensor_scalar_mul(out=BTm[:qb, :, :], in0=BT[:qb, :, :], scalar1=sqm[:qb, i:i + 1]),
                        reads=[b_BT, b_const], writes=[b_BTm])
                    add("dve", lambda e, i=i: e.tensor_tensor(out=Sn[:], in0=Sn[:], in1=dec4[:, :, i:i + 1].to_broadcast([128, 16, 128]), op=ALU.mult),
                        reads=[b_Sn, b_dec4], writes=[b_Sn])
                    for g in range(NG):
                        pi = pbank()
                        for jj in range(4):
                            c = 4 * g + jj
                            mm(psb[pi][:, jj * 128:(jj + 1) * 128], xdtd[:qb, c * 128:(c + 1) * 128], BTm[:qb, g, :], True, True,
                               reads=[b_xdtd, b_BTm], writes=[b_ps[pi]])
                        add("dve", lambda e, g=g, pi=pi: e.tensor_tensor(out=Sn[:, 4 * g:4 * g + 4, :], in0=Sn[:, 4 * g:4 * g + 4, :],
                                                                          in1=psb[pi][:, :].rearrange("p (j n) -> p j n", j=4), op=ALU.add),
                            reads=[b_Sn, b_ps[pi]], writes=[b_Sn])
                    add("sp", lambda e, i=i: e.dma_start(out=st_out_ap[i].rearrange("(c p) n -> p c n", p=128), in_=Sn[:]),
                        reads=[b_Sn], writes=[b_out], dma=True)
            if not state_only:
                pi = pbank()
                for g in range(NG):
                    mm(psb[pi][:qb, g * 128:g * 128 + qb], xc[:, 16 + g, sl], xc[:, 20 + g, sl], True, True,
                       reads=[b_xc[16 + g], b_xc[20 + g]], writes=[b_ps[pi]])
                add("dve", lambda e, pi=pi: e.tensor_tensor(out=CBm[:qb, :, :qb],
                                                            in0=psb[pi][:qb, :].rearrange("p (g n) -> p g n", g=NG)[:, :, :qb],
                                                            in1=Um[:qb, :qb].unsqueeze(1).to_broadcast([qb, NG, qb]), op=ALU.mult),
                    reads=[b_ps[pi], b_const], writes=[b_CBm])
                if interleave:
                    pump()
                for g in range(NG):
                    gi = nxt("g", 2)
                    add("dve", lambda e, gi=gi, g=g, b=b: e.tensor_tensor(
                        out=rhsA[gi][:qb, :, :qb],
                        in0=aT[:qb, b, 8 * g:8 * g + 8].unsqueeze(2).to_broadcast([qb, 8, qb]),
                        in1=Um[:qb, :qb].unsqueeze(1).to_broadcast([qb, 8, qb]), op=ALU.mult),
                        reads=[b_aT, b_const], writes=[b_rhsA[gi]])
                    hper = 512 // qb if qb < 128 else 4
                    hper = min(8, hper)
                    for h0 in range(0, 8, hper):
                        pi = pbank()
                        mm(psb[pi][:qb, :hper * qb].rearrange("p (h n) -> p h n", h=hper), Lm[:qb, :qb], rhsA[gi][:qb, h0:h0 + hper, :qb],
                           True, True, reads=[b_const, b_rhsA[gi]], writes=[b_ps[pi]])
                        add("act", lambda e, pi=pi, gi=gi, h0=h0, hper=hper: e.activation(
                            out=eseg[gi][:qb, h0:h0 + hper, :qb], in_=psb[pi][:qb, :hper * qb].rearrange("p (h n) -> p h n", h=hper), func=AF.Exp),
                            reads=[b_ps[pi]], writes=[b_eseg[gi]])
                        if interleave:
                            pump()
                    add("dve", lambda e, gi=gi, g=g: e.tensor_tensor(out=MT[gi][:qb, :, :qb], in0=eseg[gi][:qb, :, :qb],
                                                                      in1=CBm[:qb, g:g + 1, :qb].to_broadcast([qb, 8, qb]), op=ALU.mult),
                        reads=[b_eseg[gi], b_CBm], writes=[b_MT[gi]])
                    pyd = pbank()
                    for jj in range(4):
                        c = 4 * g + jj
                        mm(psb[pyd][:, jj * 128:jj * 128 + qb], ddsk[:, c, :], xc[:, c, sl], True, False,
                           reads=[b_ddsk, b_xc[c]], writes=[b_ps[pyd]])
                        for hh in range(2):
                            h = 8 * g + 2 * jj + hh
                            mm(psb[pyd][64 * hh:64 * hh + 64, jj * 128:jj * 128 + qb], xdt[:qb, h * HP:(h + 1) * HP], MT[gi][:qb, 2 * jj + hh, :qb],
                               False, True, reads=[b_xdt, b_MT[gi]], writes=[b_ps[pyd]])
                    if nsq > 1:
                        pyo = 4 + g
                    else:
                        pyo = pbank()
                        for jj in range(4):
                            c = 4 * g + jj
                            mm(psb[pyo][:, jj * 128:jj * 128 + qb], Sb_[:, c * 128:(c + 1) * 128], xc[:, 20 + g, sl], True, True,
                               reads=[b_Sb, b_xc[20 + g]], writes=[b_ps[pyo]])
                    pE = pbank()
                    for jj in range(4):
                        c = 4 * g + jj
                        mm(psb[pE][:, jj * 128:jj * 128 + qb], abc[:qb, c * 128:(c + 1) * 128], Um[:qb, :qb], True, True,
                           reads=[b_abc, b_const], writes=[b_ps[pE]])
                    ei = nxt("e", 2)

                    def v3(t):
                        return t.rearrange("p (j n) -> p j n", j=4)[:, :, :qb]

                    add("act", lambda e, ei=ei, pE=pE: e.activation(out=Et[ei][:, :, :qb], in_=v3(psb[pE][:, :]), func=AF.Exp),
                        reads=[b_ps[pE]], writes=[b_Et[ei]])
                    add("dve", lambda e, ei=ei, pyo=pyo: e.tensor_tensor(out=t1[ei][:, :, :qb], in0=v3(psb[pyo][:, :]), in1=Et[ei][:, :, :qb], op=ALU.mult),
                        reads=[b_ps[pyo], b_Et[ei]], writes=[b_t1[ei]])
                    add("dve", lambda e, ei=ei, pyd=pyd: e.tensor_tensor(out=t1[ei][:, :, :qb], in0=v3(psb[pyd][:, :]), in1=t1[ei][:, :, :qb], op=ALU.add),
                        reads=[b_ps[pyd], b_t1[ei]], writes=[b_t1[ei]])
                    add("dve", lambda e, ei=ei, g=g, sl=sl: e.tensor_tensor(out=Zreg[:, 4 * g:4 * g + 4, sl], in0=t1[ei][:, :, :qb],
                                                                             in1=Zreg[:, 4 * g:4 * g + 4, sl], op=ALU.mult),
                        reads=[b_t1[ei]] + b_Z[4 * g:4 * g + 4], writes=b_Z[4 * g:4 * g + 4])
                    if interleave:
                        pump()
            if nsq > 1:
                PBN[0] = 6
                continue
            pi = pbank()
            for c in range(16):
                mm(psb[pi][:, c:c + 1], abc[:qb, c * 128:(c + 1) * 128], onesf[:qb, 0:1], True, True,
                   reads=[b_abc, b_const], writes=[b_ps[pi]])
            add("act", lambda e, pi=pi: e.activation(out=dec[:, :], in_=psb[pi][:, :16], func=AF.Exp), reads=[b_ps[pi]], writes=[b_dec])
            if state_only:
                add("dve", lambda e: e.tensor_tensor(out=dtot[:], in0=dtot[:], in1=dec[:, :], op=ALU.mult), reads=[b_dtot, b_dec], writes=[b_dtot])
            add("dve", lambda e: e.tensor_tensor(out=Sn[:], in0=Sn[:], in1=dec[:, :].unsqueeze(2).to_broadcast([128, 16, 128]), op=ALU.mult),
                reads=[b_Sn, b_dec], writes=[b_Sn])
            for g in range(NG):
                pi = pbank()
                for jj in range(4):
                    c = 4 * g + jj
                    mm(psb[pi][:, jj * 128:(jj + 1) * 128], xdtd[:qb, c * 128:(c + 1) * 128], BT[:qb, g, :], True, True,
                       reads=[b_xdtd, b_BT], writes=[b_ps[pi]])
                add("dve", lambda e, g=g, pi=pi: e.tensor_tensor(out=Sn[:, 4 * g:4 * g + 4, :], in0=Sn[:, 4 * g:4 * g + 4, :],
                                                                  in1=psb[pi][:, :].rearrange("p (j n) -> p j n", j=4), op=ALU.add),
                    reads=[b_Sn, b_ps[pi]], writes=[b_Sn])
        if interleave:
            WBN[0] = NWB
            drain()
            PBN[0] = 6
            if next_x is not None and PREFETCH:
                nx_ap, nx_key = next_x
                add("pool", lambda e: e.dma_start(out=gA[:, :, :nt], in_=nx_ap), reads=[xsc_buf(nx_key)], writes=b_gA, dma=True)
                PREF[0] = True
                PAR[0] ^= 1
        if last and st_out_ap is not None and do_a and nsq == 1:
            add("sp", lambda e: e.dma_start(out=st_out_ap.rearrange("(c p) n -> p c n", p=128), in_=Sn[:]), reads=[b_Sn], writes=[b_out], dma=True)

        if not do_a:
            return
        if True:
            stage(7)
            for g in range(NG):
                pi = pbank()
                pend_r = []
                for jj in range(4):
                    c = 4 * g + jj
                    rb = nxt("rb", 2)
                    add("act", lambda e, c=c, rb=rb: e.activation(out=rotb[rb][:, :nt], in_=Zreg[:, c, :nt], func=AF.Square),
                        reads=[b_Z[c]], writes=[b_rotb[rb]])
                    if pend_r:
                        pend_r.pop()()

                    def mm_r(jj=jj, rb=rb, pi=pi):
                        mm(psb[pi][:, :nt], onesb[:], rotb[rb][:, :nt], jj == 0, jj == 3, reads=[b_const, b_rotb[rb]], writes=[b_ps[pi]])

                    pend_r.append(mm_r)
                while pend_r:
                    pend_r.pop()()
                ri = nxt("rf", 4)
                add("act", lambda e, pi=pi, ri=ri: e.activation(out=rotf[ri][:, :nt], in_=psb[pi][:, :nt], func=AF.Ln, bias=EPS, scale=1.0 / 512),
                    reads=[b_ps[pi]], writes=[b_rotf[ri]])
                add("act", lambda e, ri=ri: e.activation(out=rotf[ri][:, :nt], in_=rotf[ri][:, :nt], func=AF.Exp, scale=-0.5),
                    reads=[b_rotf[ri]], writes=[b_rotf[ri]])
                for jj in range(4):
                    c = 4 * g + jj
                    add("dve", lambda e, c=c, ri=ri: e.scalar_tensor_tensor(out=Zreg[:, c, :nt], in0=Zreg[:, c, :nt], scalar=t_gnw[:, l, c:c + 1],
                                                                            in1=rotf[ri][:, :nt], op0=ALU.mult, op1=ALU.mult),
                        reads=[b_Z[c], b_par, b_rotf[ri]], writes=[b_Z[c]])
            for g0 in (0, 4):
                sg_i = {}

                def cons_gb(j, pi, g0=g0):
                    ri = nxt("rf", 4)
                    sg_i[j] = ri
                    add("act", lambda e, j=j, pi=pi, ri=ri: e.activation(out=rotf[ri][:, :nt], in_=psb[pi][:, :nt], func=AF.Sigmoid,
                                                                           bias=t_bmain[:, l, BI_GB + g0 + j:BI_GB + g0 + j + 1]),
                        reads=[b_ps[pi], b_par], writes=[b_rotf[ri]])

                proj_group(C_GB + g0 * 128, 4, cons_gb)
                for h2 in range(2):
                    wi, wv = wmat(w_b_out[l, :, (g0 + 2 * h2) * 128:(g0 + 2 * h2 + 2) * 128], 16, 256)
                    for j2 in range(2):
                        j = 2 * h2 + j2
                        oc = g0 + j
                        pi = pbank()
                        for k in range(16):
                            mm(psb[pi][:, :nt], wv[:, k, j2 * 128:(j2 + 1) * 128], Zreg[:, k, :nt], k == 0, k == 15,
                               reads=b_wbuf[wi] + [b_Z[k]], writes=[b_ps[pi]])
                        ri = sg_i[j]
                        add("dve", lambda e, pi=pi, ri=ri: e.tensor_tensor(out=rotf[ri][:, :nt], in0=psb[pi][:, :nt], in1=rotf[ri][:, :nt], op=ALU.mult),
                            reads=[b_ps[pi], b_rotf[ri]], writes=[b_rotf[ri]])
                        add("dve", lambda e, oc=oc, ri=ri: e.tensor_tensor(out=m1[:, oc, :nt], in0=m1[:, oc, :nt], in1=rotf[ri][:, :nt], op=ALU.add),
                            reads=[b_m1[oc], b_rotf[ri]], writes=[b_m1[oc]])

            stage(8)
            if do_a:
                ptok2 = rhsA[0][:].rearrange("p h n -> p (h n)").rearrange("p (b d) -> p b d", d=PLE)
                add("sp", lambda e: e.dma_start(out=ptok2[:qb, :nb, :], in_=p_tok_ap.rearrange("(b p) d -> p b d", p=qb)),
                    writes=[b_rhsA[0]], dma=True)
                for c in range(2):
                    pi = pbank()
                    for b in range(nb):
                        tr(psb[pi][:, b * qb:(b + 1) * qb], ptok2[:qb, b, c * 128:(c + 1) * 128], identf[:qb, :qb],
                           reads=[b_rhsA[0], b_const], writes=[b_ps[pi]])
                    add("act", lambda e, c=c, pi=pi: e.activation(out=pT[:, c, :nt], in_=psb[pi][:, :nt], func=AF.Copy),
                        reads=[b_ps[pi]], writes=[b_pT])

            add("sp", lambda e: e.dma_start(out=Ureg[:, :, :nt], in_=xsc_ap), reads=[bx], writes=b_U, dma=True)
            for g0 in (0, 4):
                sg_i = {}

                def cons_gp(j, pi, g0=g0):
                    ri = nxt("rf", 4)
                    sg_i[j] = ri
                    add("act", lambda e, j=j, pi=pi, ri=ri: e.activation(out=rotf[ri][:, :nt], in_=psb[pi][:, :nt], func=AF.Sigmoid,
                                                                           bias=t_bmain[:, l, BI_GP + g0 + j:BI_GP + g0 + j + 1]),
                        reads=[b_ps[pi], b_par], writes=[b_rotf[ri]])

                proj_group(C_GP + g0 * 128, 4, cons_gp)
                wi, wv = wmat(w_out[l, :, g0 * 128:(g0 + 4) * 128], 8, 512)
                wpi, wpv = wmat(w_ple[l, :, g0 * 128:(g0 + 4) * 128], 2, 512)
                pend_o = []
                for j in range(4):
                    oc = g0 + j
                    pp = pbank()
                    for k in range(2):
                        mm(psb[pp][:, :nt], wpv[:, k, j * 128:(j + 1) * 128], pT[:, k, :nt], k == 0, k == 1,
                           reads=b_wbuf[wpi] + [b_pT], writes=[b_ps[pp]])
                    pm = pbank()
                    for k in range(8):
                        mm(psb[pm][:, :nt], wv[:, k, j * 128:(j + 1) * 128], m1[:, k, :nt], k == 0, k == 7,
                           reads=b_wbuf[wi] + [b_m1[k]], writes=[b_ps[pm]])
                    if pend_o:
                        pend_o.pop()()
                    ri = sg_i[j]
                    add("dve", lambda e, pp=pp, ri=ri: e.tensor_tensor(out=rotf[ri][:, :nt], in0=psb[pp][:, :nt], in1=rotf[ri][:, :nt], op=ALU.mult),
                        reads=[b_ps[pp], b_rotf[ri]], writes=[b_rotf[ri]])
                    add("dve", lambda e, pm=pm, ri=ri: e.tensor_tensor(out=rotf[ri][:, :nt], in0=psb[pm][:, :nt], in1=rotf[ri][:, :nt], op=ALU.add),
                        reads=[b_ps[pm], b_rotf[ri]], writes=[b_rotf[ri]])
                    add("dve", lambda e, oc=oc, ri=ri: e.scalar_tensor_tensor(out=Ureg[:, oc, :nt], in0=Ureg[:, oc, :nt], scalar=ALPHA, in1=rotf[ri][:, :nt],
                                                                              op0=ALU.mult, op1=ALU.add),
                        reads=[b_U[oc], b_rotf[ri]], writes=[b_U[oc]])
                    r2 = nxt("rf", 4)
                    add("act", lambda e, oc=oc, r2=r2: e.activation(out=rotf[r2][:, :nt], in_=Ureg[:, oc, :nt], func=AF.Square),
                        reads=[b_U[oc]], writes=[b_rotf[r2]])
                    def stats_o(oc=oc, r2=r2):
                        mm(psb[6][:, :nt], onesf[:], Ureg[:, oc, :nt], oc == 0, oc == 7, reads=[b_const, b_U[oc]], writes=[b_ps[6]])
                        mm(psb[7][:, :nt], onesf[:], rotf[r2][:, :nt], oc == 0, oc == 7, reads=[b_const, b_rotf[r2]], writes=[b_ps[7]])

                    pend_o.append(stats_o)
                while pend_o:
                    pend_o.pop()()
            ln_stats(6, 7, D, nt)
            for c in range(8):
                ri = nxt("rf", 4)
                add("dve", lambda e, c=c, ri=ri: e.tensor_tensor(out=rotf[ri][:, :nt], in0=Ureg[:, c, :nt], in1=st_rstd[:, :nt], op=ALU.mult),
                    reads=[b_U[c], b_rstd], writes=[b_rotf[ri]])
                add("dve", lambda e, ri=ri: e.tensor_tensor(out=rotf[ri][:, :nt], in0=rotf[ri][:, :nt], in1=st_nmr[:, :nt], op=ALU.add),
                    reads=[b_rotf[ri], b_nmr], writes=[b_rotf[ri]])
                add("act", lambda e, c=c, ri=ri: e.activation(out=Ureg[:, c, :nt], in_=rotf[ri][:, :nt], func=AF.Identity,
                                                              bias=t_lnb[:, l, c:c + 1], scale=t_lng[:, l, c:c + 1]),
                    reads=[b_rotf[ri], b_par], writes=[b_U[c]])
            if l < L - 1:
                stage(0.3)
                add("sp", lambda e: e.dma_start(out=xsc_ap, in_=Ureg[:, :, :nt]), reads=b_U, writes=[bx], dma=True)
            else:
                for b in range(nb):
                    for h2 in range(2):
                        pi = pbank()
                        for cc in range(4):
                            c = h2 * 4 + cc
                            tr(psb[pi][:qb, cc * 128:(cc + 1) * 128], Ureg[:, c, b * qb:(b + 1) * qb], identf[:],
                               reads=[b_U[c], b_const], writes=[b_ps[pi]])
                        add("act", lambda e, pi=pi, b=b, h2=h2: e.activation(out=stg[:qb, b, h2 * 512:(h2 + 1) * 512], in_=psb[pi][:qb, :512], func=AF.Copy),
                            reads=[b_ps[pi]], writes=b_Z)
                add("sp", lambda e: e.dma_start(out=y_tok_ap.rearrange("(b p) d -> p b d", p=qb), in_=stg[:qb, :nb, :]),
                    reads=b_Z, writes=[b_out], dma=True)

    ntp = Tp // NT
    NHL = KA - 1
    grp = [list(g) for g in groups]

    batched = (n_samp * Ts == 128)

    def sample_tiles(l, s0, s1):
        if batched:
            if s0 != 0:
                return
            fl = lambda ap: ap.rearrange("s t d -> (s t) d")
            tile_layer(l, n_samp * Ts, n_samp * Ts, fl(xs_in), fl(ps_in[l]), xsc_s, ("s", 0), fl(ys), True, True,
                       fl(ca_in[l]), fl(cb_in[l]), st_in[l], fl(cas_o[l]), fl(cbs_o[l]), sts_o[l], nsq=n_samp)
            return
        for s in range(s0, s1):
            tile_layer(l, Ts, Ts, xs_in[s], ps_in[l, s], xsc_s[:, :, s * Ts:(s + 1) * Ts], ("s", s), ys[s], True, True,
                       ca_in[l, s], cb_in[l, s], st_in[l, s], cas_o[l, s], cbs_o[l, s], sts_o[l, s])

    def prompt_tiles(l, mode):
        for t in range(ntp):
            t0 = t * NT
            tile_layer(l, NT, 128, xp_in[t0:t0 + NT, :], pp_in[l, t0:t0 + NT, :], xsc_p[:, :, t0:t0 + NT], ("p", t),
                       yp[t0:t0 + NT, :], t == 0, t == ntp - 1,
                       None, None, None, cap_o[l], cbp_o[l], stp_o[l], mode=mode,
                       xcs_ap=(xcs[t] if seg else None), seg=seg,
                       next_x=((xsc_p[:, :, t0 + NT:t0 + 2 * NT], ("p", t + 1)) if (mode == "p2" and t + 1 < ntp) else None))

    try:
        if seg and ntp > 0:
            add("sp", lambda e: e.dma_start(out=stg[:NHL, 0, :], in_=xprev), writes=b_Z, dma=True)
            pi = pbank()
            for c in range(8):
                tr(psb[pi][:, c * 32:c * 32 + NHL], stg[:NHL, 0, c * 128:(c + 1) * 128], identf[:NHL, :NHL],
                   reads=b_Z + [b_const], writes=[b_ps[pi]])
            add("act", lambda e, pi=pi: e.activation(out=xh_b[:, :, 0:NHL],
                                                     in_=psb[pi][:, :256].rearrange("p (c t) -> p c t", c=8)[:, :, 0:NHL], func=AF.Copy),
                reads=[b_ps[pi]], writes=[b_xh])
        for l in range(L if KSTOP >= 99 else 1):
            h = n_samp // 2
            if not seg:
                if ntp > 0:
                    prompt_tiles(l, "full")
                sample_tiles(l, 0, n_samp)
                continue
            prompt_tiles(l, "p1")
            add("sp", lambda e: e.dma_start(out=st_src[:, 0:DI], in_=Sn[:].rearrange("p c n -> p (c n)")), reads=[b_Sn], writes=[b_stsrc], dma=True)
            add("sp", lambda e: e.dma_start(out=dt_src[:, :], in_=dtot[:]), reads=[b_dtot], writes=[b_dtsrc], dma=True)
            add("pool", lambda e: e.collective_compute("AllGather", ALU.bypass, replica_groups=grp, ins=[st_src], outs=[st_all]),
                reads=[b_stsrc], writes=[b_stall], cc=True)
            add("pool", lambda e: e.collective_compute("AllGather", ALU.bypass, replica_groups=grp, ins=[dt_src], outs=[dt_all]),
                reads=[b_dtsrc], writes=[b_dtall], cc=True)
            if batched:
                if l == 0:
                    sample_tiles(0, 0, n_samp)
            else:
                sample_tiles(l, 0, h)
            add("dve", lambda e: e.memset(Sn[:], 0.0), writes=[b_Sn])
            for r in range(G - 1):
                add("sp", lambda e, r=r: e.dma_start(out=abc[:, :], in_=st_all[r * 128:(r + 1) * 128, 0:DI]), reads=[b_stall], writes=[b_abc], dma=True)
                add("sp", lambda e, r=r: e.dma_start(out=dr[:], in_=dt_all[r * 128:(r + 1) * 128, :]), reads=[b_dtall], writes=[b_dr], dma=True)
                add("dve", lambda e, r=r: e.tensor_scalar(out=ar[:], in0=dr[:], scalar1=mH[:, r:r + 1], scalar2=mHc[:, r:r + 1],
                                                           op0=ALU.mult, op1=ALU.add), reads=[b_dr, b_msk], writes=[b_ar])
                add("dve", lambda e: e.tensor_tensor(out=Sn[:], in0=Sn[:], in1=ar[:, :].unsqueeze(2).to_broadcast([128, 16, 128]), op=ALU.mult),
                    reads=[b_Sn, b_ar], writes=[b_Sn])
                add("dve", lambda e, r=r: e.scalar_tensor_tensor(out=Sn[:], in0=abc[:, :].rearrange("p (c n) -> p c n", c=16), scalar=mH[:, r:r + 1],
                                                                  in1=Sn[:], op0=ALU.mult, op1=ALU.add),
                    reads=[b_abc, b_msk, b_Sn], writes=[b_Sn])
            prompt_tiles(l, "p2")
            if l < L - 1:
                add("sp", lambda e: e.dma_start(out=xh_src.rearrange("p (c t) -> p c t", c=8), in_=Ureg[:, :, NT - NHL:NT]),
                    reads=b_U, writes=[b_xhsrc], dma=True)
                add("pool", lambda e: e.collective_compute("AllGather", ALU.bypass, replica_groups=grp, ins=[xh_src], outs=[xh_all]),
                    reads=[b_xhsrc], writes=[b_xhall], cc=True)
            if batched:
                if l < L - 1:
                    sample_tiles(l + 1, 0, n_samp)
            else:
                sample_tiles(l, h, n_samp)
            if l < L - 1:
                Gv = abc[:, :G * 8 * NHL].rearrange("p (r f) -> p r f", r=G)
                add("sp", lambda e: e.dma_start(out=Gv, in_=xh_all.rearrange("(r p) f -> p r f", p=128)), reads=[b_xhall], writes=[b_abc], dma=True)
                add("dve", lambda e: e.tensor_scalar_mul(out=xhf[:], in0=Gv[:, 0, :], scalar1=msel[:, 0:1]), reads=[b_abc, b_msk], writes=[b_xhf])
                for r in range(1, G):
                    add("dve", lambda e, r=r: e.scalar_tensor_tensor(out=xhf[:], in0=Gv[:, r, :], scalar=msel[:, r:r + 1], in1=xhf[:],
                                                                      op0=ALU.mult, op1=ALU.add), reads=[b_abc, b_msk, b_xhf], writes=[b_xhf])
                add("dve", lambda e: e.tensor_copy(out=xh_b[:, :, 0:NHL], in_=xhf[:].rearrange("p (c t) -> p c t", c=8)),
                    reads=[b_xhf], writes=[b_xh])
    except _Stop:
        pass

    run_engine, sems, last = S.emit(lambda n: es.enter_context(nc.semaphore(n)))
    with nc.Block() as block:
        @block.sync
        def _(e):
            run_engine("sp", e)
            for sn, v in last.items():
                e.wait_ge(sems[sn], v)

        @block.tensor
        def _(e):
            run_engine("pe", e)

        @block.scalar
        def _(e):
            run_engine("act", e)

        @block.vector
        def _(e):
            run_engine("dve", e)

        @block.gpsimd
        def _(e):
            run_engine("pool", e)
    es.close()
    return nc, len(S.ops)


def _chunked(v, L):
    C = v.shape[1]
    return np.ascontiguousarray(v.reshape(L, C // 128, 128).transpose(2, 0, 1))


def prep_shared(w, L):
    f = np.float32
    b_in = w["b_in"]
    cols = []
    for c0, n in ((C_VAL, 8), (C_GLU, 8), (C_GATE, 8), (C_Z, 16), (C_XBC, 24), (C_GA, 8), (C_GB, 8), (C_GP, 8)):
        cols.append(b_in[:, c0:c0 + n * 128])
    bm = np.concatenate(cols, axis=1)
    d = {
        "w_in": w["w_in"], "w_a_out": w["w_a_out"], "w_b_out": w["w_b_out"], "w_out": w["w_out"], "w_ple": w["w_ple"],
        "b_main": _chunked(bm, L),
        "b_dt": np.ascontiguousarray(b_in[:, C_DT:C_DT + NH]),
        "dt_bias": w["dt_bias"], "a_log": w["a_log"],
        "caw": np.ascontiguousarray(w["conv_a_w"].reshape(L, KA, 8, 128).transpose(3, 0, 2, 1)),
        "cab": _chunked(w["conv_a_b"], L), "nag": _chunked(w["norm_a_g"], L), "nab": _chunked(w["norm_a_b"], L),
        "cbw": np.ascontiguousarray(w["conv_b_w"].reshape(L, KB, 24, 128).transpose(3, 0, 2, 1)),
        "cbb": _chunked(w["conv_b_b"], L), "gnw": _chunked(w["gnorm_w"], L),
        "lng": _chunked(w["ln_g"], L), "lnb": _chunked(w["ln_b"], L),
        "dsk": _chunked(np.repeat(w["d_skip"], HP, axis=1), L),
        "c_ident": np.eye(128, dtype=f), "c_ones": np.ones((128, 128), f),
        "c_U": np.triu(np.ones((128, 128), f)), "c_Lw": np.tril(np.ones((128, 128), f), -1),
        "c_Ubd": np.triu(np.ones((128, 128), f)) * np.kron(np.eye(4, dtype=f), np.ones((32, 32), f)),
        "c_Lwbd": np.tril(np.ones((128, 128), f), -1) * np.kron(np.eye(4, dtype=f), np.ones((32, 32), f)),
        "c_sqm": np.kron(np.eye(4, dtype=f), np.ones((32, 1), f)),
    }
    return {k: np.ascontiguousarray(v, dtype=f) for k, v in d.items()}


_CACHE = {}


def run_sharded(inp, ncores=8, nseg=4, trace=False):
    f = np.float32
    x_prompt, x_sample, p_prompt, p_sample = inp["x_prompt"], inp["x_sample"], inp["p_prompt"], inp["p_sample"]
    cache_conv_a, cache_conv_b, state_ssm = inp["cache_conv_a"], inp["cache_conv_b"], inp["state_ssm"]
    L = inp["w_in"].shape[0]
    Bp, Tp, _ = x_prompt.shape
    Bs, Ts, _ = x_sample.shape
    assert Bp * nseg == ncores and Bs % ncores == 0 and Tp % (nseg * 512) == 0
    n_samp = Bs // ncores
    Tseg = Tp // nseg
    shared = prep_shared(inp, L)
    groups = tuple(tuple(range(b * nseg, (b + 1) * nseg)) for b in range(Bp))
    key = (Tseg, n_samp, Ts, L, groups)
    if key not in _CACHE:
        _CACHE[key] = build_program(Tseg, n_samp, Ts, L, seg=True, groups=groups)[0]
    nc = _CACHE[key]
    st = state_ssm.reshape(L, Bs, DI, NS)
    in_maps = []
    for c in range(ncores):
        b, k = divmod(c, nseg)
        t0 = k * Tseg
        sl = slice(c * n_samp, (c + 1) * n_samp)
        xprev = x_prompt[b, t0 - (KA - 1):t0] if k > 0 else np.zeros((KA - 1, D), f)
        msel = np.zeros((128, 4), f)
        if k > 0:
            msel[:, k - 1] = 1.0
        mH = np.zeros((128, 4), f)
        mH[:, :k] = 1.0
        m = dict(shared)
        m.update({
            "xp_in": x_prompt[b, t0:t0 + Tseg], "pp_in": p_prompt[:, b, t0:t0 + Tseg], "xprev": xprev,
            "hflag_in": np.full((128, 32), 1.0 if k > 0 else 0.0, f), "msel_in": msel, "mH_in": mH, "mHc_in": 1.0 - mH,
            "xs_in": x_sample[sl], "ps_in": p_sample[:, sl],
            "ca_in": cache_conv_a[:, sl], "cb_in": cache_conv_b[:, sl], "st_in": st[:, sl],
        })
        in_maps.append({kk: np.ascontiguousarray(v, dtype=f) for kk, v in m.items()})
    res = run_bass_kernel_spmd(nc, in_maps, core_ids=list(range(ncores)), **({"trace": True} if trace else {}))
    R = res.results
    y_prompt = np.stack([np.concatenate([R[b * nseg + k]["yp"] for k in range(nseg)], 0) for b in range(Bp)], 0)
    y_sample = np.concatenate([R[c]["ys"] for c in range(ncores)], 0)
    lastc = [b * nseg + nseg - 1 for b in range(Bp)]
    ca_p = np.stack([R[c]["cap_o"] for c in lastc], 1)
    cb_p = np.stack([R[c]["cbp_o"] for c in lastc], 1)
    st_p = np.stack([R[c]["stp_o"] for c in lastc], 1).reshape(L, Bp, NH, HP, NS)
    ca_s = np.concatenate([R[c]["cas_o"] for c in range(ncores)], 1)
    cb_s = np.concatenate([R[c]["cbs_o"] for c in range(ncores)], 1)
    st_s = np.concatenate([R[c]["sts_o"] for c in range(ncores)], 1).reshape(L, Bs, NH, HP, NS)
    out = (y_prompt, y_sample, ca_p, cb_p, st_p, ca_s, cb_s, st_s)
    return tuple(np.ascontiguousarray(o, dtype=f) for o in out), res


def kernel(x_prompt, x_sample, cache_conv_a, cache_conv_b, state_ssm, p_prompt, p_sample,
           w_in, b_in, conv_a_w, conv_a_b, norm_a_g, norm_a_b, w_a_out,
           conv_b_w, conv_b_b, dt_bias, a_log, d_skip, gnorm_w, w_b_out, w_out, w_ple, ln_g, ln_b):
    A = lambda v: np.asarray(v, dtype=np.float32)
    inp = dict(x_prompt=A(x_prompt), x_sample=A(x_sample), cache_conv_a=A(cache_conv_a), cache_conv_b=A(cache_conv_b),
               state_ssm=A(state_ssm), p_prompt=A(p_prompt), p_sample=A(p_sample),
               w_in=A(w_in), b_in=A(b_in), conv_a_w=A(conv_a_w), conv_a_b=A(conv_a_b), norm_a_g=A(norm_a_g), norm_a_b=A(norm_a_b),
               w_a_out=A(w_a_out), conv_b_w=A(conv_b_w), conv_b_b=A(conv_b_b), dt_bias=A(dt_bias), a_log=A(a_log), d_skip=A(d_skip),
               gnorm_w=A(gnorm_w), w_b_out=A(w_b_out), w_out=A(w_out), w_ple=A(w_ple), ln_g=A(ln_g), ln_b=A(ln_b))
    out, _ = run_sharded(inp, ncores=8, nseg=4)
    return out
```

```python
import numpy as np
import os
from contextlib import ExitStack
import concourse.bass as bass
import concourse.mybir as mybir
from concourse.bass_utils import run_bass_kernel_spmd

F32 = mybir.dt.float32
BF16 = mybir.dt.bfloat16
AF = mybir.ActivationFunctionType
ALU = mybir.AluOpType

D = 1024
DEPTH = 4
DA = 1024
DI = 2048
NH = 32
HP = 64
NG = 4
NS = 128
DXBC = 3072
PLE = 256
DIN = 11296
KA = 31
KB = 4
ALPHA = float((2 * DEPTH) ** 0.25)
EPS = 1e-5
C_VAL, C_GLU, C_GATE, C_Z, C_XBC, C_DT, C_GA, C_GB, C_GP = 0, 1024, 2048, 3072, 5120, 8192, 8224, 9248, 10272
BI_VAL, BI_GLU, BI_GATE, BI_Z, BI_XBC, BI_GA, BI_GB, BI_GP = 0, 8, 16, 24, 40, 64, 72, 80

KSTOP = float(os.environ.get("KSTOP", "99"))
INTERLEAVE = os.environ.get("KINTER", "1") == "1"
PREFETCH = os.environ.get("KPREF", "1") == "1"


class _Stop(Exception):
    pass


def stage(k):
    if k > KSTOP:
        raise _Stop()


COMPUTE = ("pe", "act", "dve", "pool")
ALLQ = COMPUTE + ("sp",)


class Buf:
    __slots__ = ("name", "w", "r", "excl")

    def __init__(self, name, excl=False):
        self.name = name
        self.w = None
        self.r = []
        self.excl = excl


class Op:
    __slots__ = ("eng", "fn", "deps", "is_dma", "needed", "idx", "semval", "sem", "cc")

    def __init__(self, eng, fn, is_dma):
        self.eng = eng
        self.fn = fn
        self.deps = set()
        self.is_dma = is_dma
        self.cc = False
        self.needed = False
        self.semval = None
        self.sem = None


class Sched:
    def __init__(self, n_dma_sems=8, max_sem=32000):
        self.ops = []
        self.n_dma_sems = n_dma_sems
        self.max_sem = max_sem

    def add(self, eng, fn, reads=(), writes=(), dma=False, cc=False):
        dma = dma or cc
        op = Op(eng, fn, dma)
        op.cc = cc
        op.idx = len(self.ops)
        writes = list(writes) + [b for b in reads if b.excl]
        reads = [b for b in reads if not b.excl]
        for b in reads:
            if b.w is not None:
                op.deps.add(b.w)
        for b in writes:
            if b.w is not None:
                op.deps.add(b.w)
            for r in b.r:
                op.deps.add(r)
        for b in reads:
            if not dma:
                b.r = [r for r in b.r if self.ops[r].is_dma or self.ops[r].eng != eng]
            b.r.append(op.idx)
        for b in writes:
            b.w = op.idx
            b.r = []
        self.ops.append(op)
        return op

    def emit(self, semctx):
        ops = self.ops
        for op in ops:
            keep = set()
            for d in op.deps:
                p = ops[d]
                if d == op.idx:
                    continue
                if p.eng == "pe" and op.eng == "pe" and not p.is_dma and not op.is_dma:
                    continue
                keep.add(d)
            op.deps = keep
        eng_sem_idx = {e: 0 for e in ALLQ}
        eng_cnt = {e: 0 for e in ALLQ}
        dma_rr = {e: 0 for e in ALLQ}
        dma_cnt = {}
        dma_gen = {}
        prev_dma_on_sem = {}
        for op in ops:
            if op.cc:
                op.sem = f"cc_{op.idx}"
                op.semval = 1
            elif op.is_dma:
                k = dma_rr[op.eng] % self.n_dma_sems
                dma_rr[op.eng] += 1
                base = f"d_{op.eng}_{k}"
                gen = dma_gen.get(base, 0)
                c = dma_cnt.get(base, 0) + 16
                if c > self.max_sem:
                    gen += 1
                    dma_gen[base] = gen
                    c = 16
                dma_cnt[base] = c
                op.sem = f"{base}_{gen}"
                op.semval = c
                pv = prev_dma_on_sem.get(base)
                if pv is not None:
                    op.deps.add(pv)
                prev_dma_on_sem[base] = op.idx
        for op in ops:
            for d in op.deps:
                ops[d].needed = True
        for op in ops:
            if (not op.is_dma) and op.needed:
                if eng_cnt[op.eng] >= self.max_sem:
                    eng_sem_idx[op.eng] += 1
                    eng_cnt[op.eng] = 0
                eng_cnt[op.eng] += 1
                op.sem = f"e_{op.eng}_{eng_sem_idx[op.eng]}"
                op.semval = eng_cnt[op.eng]
        names = sorted({op.sem for op in ops if op.sem is not None})
        sems = {n: semctx(n) for n in names}
        per_eng = {e: [] for e in ALLQ}
        for op in ops:
            per_eng[op.eng].append(op)

        def run_engine(ename, eh):
            waited = {}
            for op in per_eng[ename]:
                need = {}
                for d in op.deps:
                    p = ops[d]
                    if p.semval is None:
                        continue
                    if need.get(p.sem, 0) < p.semval:
                        need[p.sem] = p.semval
                for sname, v in need.items():
                    if waited.get(sname, 0) >= v:
                        continue
                    eh.wait_ge(sems[sname], v)
                    waited[sname] = v
                ins = op.fn(eh)
                if op.sem is not None:
                    if op.cc:
                        ins.then_inc(sems[op.sem])
                    else:
                        ins.then_inc(sems[op.sem], 16 if op.is_dma else 1)

        last = {}
        for op in ops:
            if op.is_dma and not op.cc:
                last[op.sem] = max(last.get(op.sem, 0), op.semval)
        return run_engine, sems, last


def build_program(Tp, n_samp, Ts=32, L=DEPTH, NTP=512, seg=True, groups=((0, 1, 2, 3), (4, 5, 6, 7))):
    nc = bass.Bass("TRN2", target_bir_lowering=False)
    S = Sched()
    es = ExitStack()

    def din(name, shape):
        return nc.dram_tensor(name, list(shape), F32, kind="ExternalInput").ap()

    def dout(name, shape):
        return nc.dram_tensor(name, list(shape), F32, kind="ExternalOutput").ap()

    xp_in = din("xp_in", [Tp, D])
    pp_in = din("pp_in", [L, Tp, PLE])
    xs_in = din("xs_in", [n_samp, Ts, D])
    ps_in = din("ps_in", [L, n_samp, Ts, PLE])
    ca_in = din("ca_in", [L, n_samp, KA - 1, DA])
    cb_in = din("cb_in", [L, n_samp, KB - 1, DXBC])
    st_in = din("st_in", [L, n_samp, DI, NS])
    w_in = din("w_in", [L, D, DIN])
    w_a_out = din("w_a_out", [L, DA, D])
    w_b_out = din("w_b_out", [L, DI, D])
    w_out = din("w_out", [L, D, D])
    w_ple = din("w_ple", [L, PLE, D])
    b_main = din("b_main", [128, L, 88])
    b_dt = din("b_dt", [L, NH])
    dt_bias = din("dt_bias", [L, NH])
    a_log = din("a_log", [L, NH])
    caw = din("caw", [128, L, 8, KA])
    cab = din("cab", [128, L, 8])
    nag = din("nag", [128, L, 8])
    nab = din("nab", [128, L, 8])
    cbw = din("cbw", [128, L, 24, KB])
    cbb = din("cbb", [128, L, 24])
    gnw = din("gnw", [128, L, 16])
    lng = din("lng", [128, L, 8])
    lnb = din("lnb", [128, L, 8])
    dsk = din("dsk", [128, L, 16])
    c_ident = din("c_ident", [128, 128])
    c_ones = din("c_ones", [128, 128])
    c_U = din("c_U", [128, 128])
    c_Lw = din("c_Lw", [128, 128])
    xprev = din("xprev", [KA - 1, D])
    c_Ubd = din("c_Ubd", [128, 128])
    c_Lwbd = din("c_Lwbd", [128, 128])
    c_sqm = din("c_sqm", [128, 4])
    hflag_in = din("hflag_in", [128, 32])
    msel_in = din("msel_in", [128, 4])
    mH_in = din("mH_in", [128, 4])
    mHc_in = din("mHc_in", [128, 4])

    yp = dout("yp", [Tp, D])
    ys = dout("ys", [n_samp, Ts, D])
    cap_o = dout("cap_o", [L, KA - 1, DA])
    cbp_o = dout("cbp_o", [L, KB - 1, DXBC])
    stp_o = dout("stp_o", [L, DI, NS])
    cas_o = dout("cas_o", [L, n_samp, KA - 1, DA])
    cbs_o = dout("cbs_o", [L, n_samp, KB - 1, DXBC])
    sts_o = dout("sts_o", [L, n_samp, DI, NS])
    xsc_p = nc.dram_tensor("xsc_p", [128, 8, Tp], F32).ap()
    xsc_s = nc.dram_tensor("xsc_s", [128, 8, n_samp * Ts], F32).ap()
    G = len(groups[0])
    xcs = nc.dram_tensor("xcs", [max(1, Tp // NTP), 128, 24, NTP], BF16).ap()
    xh_src = nc.dram_tensor("xh_src", [128, 8 * (KA - 1)], F32).ap()
    xh_all = nc.dram_tensor("xh_all", [G * 128, 8 * (KA - 1)], F32).ap()
    st_src = nc.dram_tensor("st_src", [128, DI], F32).ap()
    st_all = nc.dram_tensor("st_all", [G * 128, DI], F32).ap()
    dt_src = nc.dram_tensor("dt_src", [128, 16], F32).ap()
    dt_all = nc.dram_tensor("dt_all", [G * 128, 16], F32).ap()

    def sb(name, shape, dt=F32):
        return es.enter_context(nc.sbuf_tensor(name, list(shape), dt))

    NT = NTP
    identf = sb("identf", [128, 128]); identb = sb("identb", [128, 128], BF16)
    onesf = sb("onesf", [128, 128]); onesb = sb("onesb", [128, 128], BF16)
    Uf = sb("Uf", [128, 128]); Lwf = sb("Lwf", [128, 128])
    Ubd = sb("Ubd", [128, 128]); Lwbd = sb("Lwbd", [128, 128]); sqm = sb("sqm", [128, 4])
    t_bmain = sb("t_bmain", [128, L, 88])
    t_dtb = sb("t_dtb", [128, L, NH]); t_dtb2 = sb("t_dtb2", [128, L, NH]); t_A = sb("t_A", [128, L, NH])
    t_caw = sb("t_caw", [128, L, 8, KA]); t_cab = sb("t_cab", [128, L, 8])
    t_nag = sb("t_nag", [128, L, 8]); t_nab = sb("t_nab", [128, L, 8])
    t_cbw = sb("t_cbw", [128, L, 24, KB]); t_cbb = sb("t_cbb", [128, L, 24])
    t_gnw = sb("t_gnw", [128, L, 16]); t_lng = sb("t_lng", [128, L, 8]); t_lnb = sb("t_lnb", [128, L, 8])
    t_dsk = sb("t_dsk", [128, L, 16])
    ddsk = sb("ddsk", [128, 16, 128], BF16)

    xb = sb("xb", [128, 8, NT], BF16)
    pT = sb("pT", [128, 2, NT], BF16)
    Ureg = sb("Ureg", [128, 8, NT])
    Uflat = Ureg[:].rearrange("p c t -> p (c t)")
    Zraw = sb("Zraw", [128, 8 * NT])
    Zreg = Zraw[:].bitcast(BF16).rearrange("p (c t) -> p c t", c=16)
    stg = Zraw[:].rearrange("p (a d) -> p a d", d=D)
    uext = sb("uext", [128, 8, KA - 1 + NT], BF16)
    gA = sb("gA", [128, 8, NT], BF16)
    m1 = sb("m1", [128, 8, NT], BF16)
    xc = sb("xc", [128, 24, NT], BF16)
    hB = sb("hB", [128, 24, 4 * (KB - 1)], BF16)
    NWB = 4
    wbuf = [sb(f"wbuf{i}", [128, 4096], BF16) for i in range(2)]
    wdt = sb("wdt", [128, 8, NH], BF16)
    dA = [sb(f"dA{i}", [128, KA, 128], BF16) for i in range(2)]
    dB = [sb(f"dB{i}", [128, KB, 128], BF16) for i in range(2)]
    xpb = [sb(f"xpb{i}", [128, KB - 1 + NT], BF16) for i in range(2)]
    rotf = [sb(f"rotf{i}", [128, NT]) for i in range(4)]
    rotb = [sb(f"rotb{i}", [128, NT], BF16) for i in range(2)]
    st_mean = sb("st_mean", [128, NT]); st_rstd = sb("st_rstd", [128, NT]); st_nmr = sb("st_nmr", [128, NT])
    st_tmp = st_nmr
    NBK = max(1, NT // 128)
    dtT = sb("dtT", [128, NBK, NH]); aT = sb("aT", [128, NBK, NH])
    dtv = dtT
    BT = sb("BT", [128, NG, 128], BF16)
    BTm = sb("BTm", [128, NG, 128], BF16)
    dec4 = sb("dec4", [128, 16, 4])
    htmp = sb("htmp", [128, 4 * (KA - 1)], BF16)
    xdx = sb("xdx", [128, 2 * DI], BF16)
    xdt = xdx[:, 0:DI]
    xdtd = xdx[:, DI:2 * DI]
    d2e = sb("d2e", [128, NH])
    abc = sb("abc", [128, DI])
    wbuf.append(abc[:].bitcast(BF16))
    wbuf.append(xdx[:])
    CBm = sb("CBm", [128, NG, 128], BF16)
    rhsA = [sb(f"rhsA{i}", [128, 8, 128]) for i in range(1)] * 2
    eseg = [sb(f"eseg{i}", [128, 8, 128], BF16) for i in range(2)]
    MT = eseg
    Et = [sb(f"Et{i}", [128, 4, 128]) for i in range(2)]
    t1 = Et
    Sn = sb("Sn", [128, 16, 128])
    Snb = xdtd.rearrange("p (c n) -> p c n", c=16)
    Sb_ = sb("Sb", [128, DI], BF16)
    dec = sb("dec", [128, 16])
    ptok = abc[:, :1024].rearrange("p (b d) -> p b d", d=PLE)
    ua = gA

    xh_b = sb("xh_b", [128, 8, 32], BF16)
    xhf = sb("xhf", [128, 8 * (KA - 1)])
    sgh = xhf[:, :128].rearrange("p (j t) -> p j t", j=4)
    hflag = sb("hflag", [128, 32])
    msel = sb("msel", [128, 4]); mH = sb("mH", [128, 4]); mHc = sb("mHc", [128, 4])
    dtot = sb("dtot", [128, 16]); dr = sb("dr", [128, 16]); ar = sb("ar", [128, 16])
    psb = [es.enter_context(nc.psum_tensor(f"psb{i}", [128, 512], F32)) for i in range(8)]

    def B(n):
        return Buf(n)

    b_const = B("const")
    b_par = B("par")
    b_ddsk = B("ddsk")
    b_xb = [B(f"xb{c}") for c in range(8)]
    b_pT = B("pT")
    b_U = [B(f"U{c}") for c in range(8)]
    b_Z = [B(f"Z{c}") for c in range(16)]
    b_uext = [B(f"uext{c}") for c in range(8)]
    b_gA = [B(f"gA{c}") for c in range(8)]
    b_m1 = [B(f"m1{c}") for c in range(8)]
    b_xc = [B(f"xc{c}") for c in range(24)]
    b_hB = [B(f"hB{c}") for c in range(24)]
    b_wbuf = [[B("wbuf0")], [B("wbuf1")]]
    b_wdt = B("wdt")
    b_dA = [B("dA0"), B("dA1")]; b_dB = [B("dB0"), B("dB1")]; b_xpb = [B("xpb0"), B("xpb1")]
    b_rotf = [B(f"rotf{i}") for i in range(4)]; b_rotb = [B(f"rotb{i}") for i in range(2)]
    b_mean = B("mean"); b_rstd = B("rstd"); b_nmr = B("nmr"); b_sttmp = b_nmr
    b_dtT = B("dtT"); b_aT = B("aT"); b_dtv = b_dtT
    b_BTm = B("BTm"); b_dec4 = B("dec4"); b_htmp = B("htmp")
    b_BT = B("BT"); b_xdt = B("xdt"); b_xdtd = B("xdtd"); b_d2e = B("d2e"); b_abc = B("abc"); b_CBm = B("CBm")
    b_wbuf.append([b_abc]); b_wbuf.append([b_xdt, b_xdtd])
    b_rhsA = [B("rhsA0")] * 2; b_eseg = [B("eseg0"), B("eseg1")]; b_MT = b_eseg
    b_Et = [B("Et0"), B("Et1")]; b_t1 = b_Et
    b_Sn = B("Sn"); b_Snb = b_xdtd; b_Sb = B("Sb"); b_dec = B("dec")
    b_ua = b_gA
    XB = [xb, gA]
    b_XB = [b_xb, b_gA]
    PAR = [0]
    PREF = [False]
    b_ps = [Buf(f"ps{i}", excl=True) for i in range(8)]
    b_out = B("out")
    b_xh = B("xh"); b_xhf = B("xhf"); b_sgh = b_xhf; b_msk = B("msk"); b_dtot = B("dtot"); b_dr = B("dr"); b_ar = B("ar")
    b_xhsrc = B("xhsrc"); b_xhall = B("xhall"); b_stsrc = B("stsrc"); b_stall = B("stall"); b_dtsrc = B("dtsrc"); b_dtall = B("dtall")
    b_xsc = {}

    def xsc_buf(key):
        if key not in b_xsc:
            b_xsc[key] = B(f"xsc{key}")
        return b_xsc[key]

    rr = {"ps": 0, "rf": 0, "rb": 0, "w": 0, "dA": 0, "dB": 0, "xp": 0, "g": 0, "e": 0}

    def nxt(key, n):
        i = rr[key] % n
        rr[key] += 1
        return i

    PBN = [6]
    WBN = [NWB]

    def pbank():
        return nxt("ps", PBN[0])

    def add(eng, fn, reads=(), writes=(), dma=False, cc=False):
        return S.add(eng, fn, reads, writes, dma, cc)

    add("sp", lambda e: e.dma_start(out=identf[:], in_=c_ident[:, :]), writes=[b_const], dma=True)
    add("sp", lambda e: e.dma_start(out=onesf[:], in_=c_ones[:, :]), writes=[b_const], dma=True)
    add("sp", lambda e: e.dma_start(out=Uf[:], in_=c_U[:, :]), writes=[b_const], dma=True)
    add("sp", lambda e: e.dma_start(out=Lwf[:], in_=c_Lw[:, :]), writes=[b_const], dma=True)
    add("sp", lambda e: e.dma_start(out=Ubd[:], in_=c_Ubd[:, :]), writes=[b_const], dma=True)
    add("sp", lambda e: e.dma_start(out=Lwbd[:], in_=c_Lwbd[:, :]), writes=[b_const], dma=True)
    add("sp", lambda e: e.dma_start(out=sqm[:], in_=c_sqm[:, :]), writes=[b_const], dma=True)
    for (t, d_) in ((hflag, hflag_in), (msel, msel_in), (mH, mH_in), (mHc, mHc_in)):
        add("sp", (lambda t, d_: lambda e: e.dma_start(out=t[:], in_=d_))(t, d_), writes=[b_msk], dma=True)
    add("dve", lambda e: e.tensor_copy(out=identb[:], in_=identf[:]), reads=[b_const], writes=[b_const])
    add("dve", lambda e: e.tensor_copy(out=onesb[:], in_=onesf[:]), reads=[b_const], writes=[b_const])
    for (t, d_) in ((t_bmain, b_main), (t_caw, caw), (t_cab, cab), (t_nag, nag), (t_nab, nab), (t_cbw, cbw),
                    (t_cbb, cbb), (t_gnw, gnw), (t_lng, lng), (t_lnb, lnb), (t_dsk, dsk)):
        add("sp", (lambda t, d_: lambda e: e.dma_start(out=t[:], in_=d_))(t, d_), writes=[b_par], dma=True)
    add("sp", lambda e: e.dma_start(out=t_dtb[:].rearrange("p l h -> p (l h)"),
                                    in_=b_dt.rearrange("l h -> (l h)").partition_broadcast(128)), writes=[b_par], dma=True)
    add("sp", lambda e: e.dma_start(out=t_dtb2[:].rearrange("p l h -> p (l h)"),
                                    in_=dt_bias.rearrange("l h -> (l h)").partition_broadcast(128)), writes=[b_par], dma=True)
    add("sp", lambda e: e.dma_start(out=t_A[:].rearrange("p l h -> p (l h)"),
                                    in_=a_log.rearrange("l h -> (l h)").partition_broadcast(128)), writes=[b_par], dma=True)
    add("dve", lambda e: e.tensor_tensor(out=t_dtb[:], in0=t_dtb[:], in1=t_dtb2[:], op=ALU.add), reads=[b_par], writes=[b_par])
    add("act", lambda e: e.activation(out=t_A[:], in_=t_A[:], func=AF.Exp), reads=[b_par], writes=[b_par])
    add("dve", lambda e: e.tensor_scalar_mul(out=t_A[:], in0=t_A[:], scalar1=-1.0), reads=[b_par], writes=[b_par])

    def mm(out, lhsT, rhs, start, stop, reads, writes):
        add("pe", lambda e: e.matmul(out, lhsT=lhsT, rhs=rhs, start=start, stop=stop), reads=reads, writes=writes)

    def tr(out, in_, ident, reads, writes):
        add("pe", lambda e: e.transpose(out, in_, ident), reads=reads, writes=writes)

    def ln_stats(pi_sum, pi_sq, n, nt):
        inv = 1.0 / n
        add("act", lambda e: e.activation(out=st_mean[:, :nt], in_=psb[pi_sum][:, :nt], func=AF.Identity, scale=inv),
            reads=[b_ps[pi_sum]], writes=[b_mean])
        add("dve", lambda e: e.tensor_tensor(out=st_tmp[:, :nt], in0=st_mean[:, :nt], in1=st_mean[:, :nt], op=ALU.mult),
            reads=[b_mean], writes=[b_sttmp])
        add("dve", lambda e: e.scalar_tensor_tensor(out=st_tmp[:, :nt], in0=psb[pi_sq][:, :nt], scalar=inv, in1=st_tmp[:, :nt],
                                                    op0=ALU.mult, op1=ALU.subtract),
            reads=[b_ps[pi_sq], b_sttmp], writes=[b_sttmp])
        add("act", lambda e: e.activation(out=st_rstd[:, :nt], in_=st_tmp[:, :nt], func=AF.Ln, bias=EPS, scale=1.0),
            reads=[b_sttmp], writes=[b_rstd])
        add("act", lambda e: e.activation(out=st_rstd[:, :nt], in_=st_rstd[:, :nt], func=AF.Exp, scale=-0.5),
            reads=[b_rstd], writes=[b_rstd])
        add("dve", lambda e: e.scalar_tensor_tensor(out=st_nmr[:, :nt], in0=st_mean[:, :nt], scalar=-1.0, in1=st_rstd[:, :nt],
                                                    op0=ALU.mult, op1=ALU.mult),
            reads=[b_mean, b_rstd], writes=[b_nmr])

    def tile_layer(l, nt, qb, x_tok_ap, p_tok_ap, xsc_ap, xsc_key, y_tok_ap, first, last,
                   ca_in_ap, cb_in_ap, st_in_ap, ca_out_ap, cb_out_ap, st_out_ap, mode="full", xcs_ap=None, seg=False, nsq=1, next_x=None):
        nb = nt // qb
        bx = xsc_buf(xsc_key)
        do_a = mode in ("p2", "full")
        do_xbc = mode in ("p1", "full")
        state_only = mode == "p1"
        xb = XB[PAR[0]]; gA = XB[1 - PAR[0]]; ua = gA
        b_xb = b_XB[PAR[0]]; b_gA = b_XB[1 - PAR[0]]; b_ua = b_gA
        tl = nt // nsq
        WA = KA - 1 + tl
        WB = KB - 1 + tl
        Um = Ubd if nsq > 1 else Uf
        Lm = Lwbd if nsq > 1 else Lwf

        def sv(ap2d, w=None):
            return ap2d.rearrange("p (s w) -> p s w", s=nsq)
        stage(0)

        if l == 0 and mode != "p2":
            add("sp", lambda e: e.dma_start(out=stg[:qb, :nb, :], in_=x_tok_ap.rearrange("(b p) d -> p b d", p=qb)),
                writes=b_Z, dma=True)
            for c in range(8):
                pi = pbank()
                for b in range(nb):
                    tr(psb[pi][:, b * qb:(b + 1) * qb], stg[:qb, b, c * 128:(c + 1) * 128], identf[:qb, :qb],
                       reads=b_Z + [b_const], writes=[b_ps[pi]])
                add("act", lambda e, c=c, pi=pi: e.activation(out=xb[:, c, :nt], in_=psb[pi][:, :nt], func=AF.Copy),
                    reads=[b_ps[pi]], writes=[b_xb[c]])
                add("dve", lambda e, c=c, pi=pi: e.tensor_copy(out=Ureg[:, c, :nt], in_=psb[pi][:, :nt]),
                    reads=[b_ps[pi]], writes=[b_U[c]])
            stage(0.3)
            add("sp", lambda e: e.dma_start(out=xsc_ap, in_=Ureg[:, :, :nt]), reads=b_U, writes=[bx], dma=True)
        elif PREF[0]:
            PREF[0] = False
        else:
            add("pool", lambda e: e.dma_start(out=xb[:, :, :nt], in_=xsc_ap), reads=[bx], writes=b_xb, dma=True)
        stage(0.6)
        stage(1)
        def proj_group(col0, nchunks, consume, extra=None):
            for _ in proj_group_g(col0, nchunks, consume, extra):
                pass

        def proj_group_g(col0, nchunks, consume, extra=None):
            for g0 in range(0, nchunks, 4):
                ng = min(4, nchunks - g0)
                wi = nxt("w", WBN[0])
                src = w_in[l, :, col0 + g0 * 128: col0 + (g0 + ng) * 128].rearrange("(k p) n -> p k n", p=128)
                view = wbuf[wi][:, :8 * ng * 128].rearrange("p (k n) -> p k n", k=8)
                add("pool", lambda e, view=view, src=src: e.dma_start(out=view, in_=src), writes=b_wbuf[wi], dma=True)
                for j in range(ng):
                    pi = pbank()
                    for k in range(8):
                        mm(psb[pi][:, :nt], view[:, k, j * 128:(j + 1) * 128], xb[:, k, :nt], k == 0, k == 7,
                           reads=b_wbuf[wi] + [b_xb[k]], writes=[b_ps[pi]])
                    if extra is None:
                        consume(g0 + j, pi)
                        yield
                    else:
                        ne = extra.shape[2]
                        pih = pbank()
                        for k in range(8):
                            mm(psb[pih][:, :ne], view[:, k, j * 128:(j + 1) * 128], extra[:, k, :], k == 0, k == 7,
                               reads=b_wbuf[wi] + [b_xh], writes=[b_ps[pih]])
                        consume(g0 + j, pi, pih)
                        yield

        def wmat(src2d, kch, ncols):
            wi = nxt("w", WBN[0])
            src = src2d.rearrange("(k p) n -> p k n", p=128)
            view = wbuf[wi][:, :kch * ncols].rearrange("p (k n) -> p k n", k=kch)
            add("pool", lambda e: e.dma_start(out=view, in_=src), writes=b_wbuf[wi], dma=True)
            return wi, view

        def gen_branch_a():
            halo_a = first and seg
            if first and not seg:
                if ca_in_ap is None:
                    add("dve", lambda e: e.memset(uext[:, :, 0:KA - 1], 0.0), writes=b_uext)
                else:
                    nh = nsq * (KA - 1)
                    add("sp", lambda e: e.dma_start(out=stg[:nh, 0, :], in_=ca_in_ap), writes=b_Z, dma=True)
                    for c0 in (0, 4):
                        pi = pbank()
                        for cc in range(4):
                            c = c0 + cc
                            tr(psb[pi][:, cc * 128:cc * 128 + nh], stg[:nh, 0, c * 128:(c + 1) * 128], identf[:nh, :nh],
                               reads=b_Z + [b_const], writes=[b_ps[pi]])
                        for cc in range(4):
                            c = c0 + cc
                            add("act", lambda e, pi=pi, c=c, cc=cc: e.activation(out=sv(uext[:, c, :nsq * WA])[:, :, 0:KA - 1],
                                                                                 in_=sv(psb[pi][:, cc * 128:cc * 128 + nh]), func=AF.Copy),
                                reads=[b_ps[pi]], writes=[b_uext[c]])
            val_ps = {}

            def cons_val(j, pi):
                val_ps[j] = pi

            for g0 in (0, 4):
                sg_i = {}

                def cons_glu(j, pi, pih=None, g0=g0):
                    ri = nxt("rf", 4)
                    sg_i[j] = ri
                    if pih is not None:
                        add("act", lambda e, j=j, pih=pih: e.activation(out=sgh[:, j, :KA - 1], in_=psb[pih][:, :KA - 1], func=AF.Sigmoid,
                                                                        bias=t_bmain[:, l, BI_GLU + g0 + j:BI_GLU + g0 + j + 1]),
                            reads=[b_ps[pih], b_par], writes=[b_sgh])
                    add("act", lambda e, j=j, pi=pi, ri=ri: e.activation(out=rotf[ri][:, :nt], in_=psb[pi][:, :nt], func=AF.Sigmoid,
                                                                           bias=t_bmain[:, l, BI_GLU + g0 + j:BI_GLU + g0 + j + 1]),
                        reads=[b_ps[pi], b_par], writes=[b_rotf[ri]])

                yield from proj_group_g(C_GLU + g0 * 128, 4, cons_glu, extra=(xh_b[:, :, 0:KA - 1] if halo_a else None))

                def cons_val2(j, pi, pih=None, g0=g0):
                    ri = sg_i[j]
                    c = g0 + j
                    if pih is not None:
                        add("dve", lambda e, c=c, j=j, pih=pih: e.scalar_tensor_tensor(
                            out=uext[:, c, 0:KA - 1], in0=psb[pih][:, :KA - 1], scalar=t_bmain[:, l, BI_VAL + c:BI_VAL + c + 1],
                            in1=sgh[:, j, :KA - 1], op0=ALU.add, op1=ALU.mult),
                            reads=[b_ps[pih], b_par, b_sgh], writes=[b_uext[c]])
                        add("dve", lambda e, c=c: e.tensor_scalar_mul(out=uext[:, c, 0:KA - 1], in0=uext[:, c, 0:KA - 1], scalar1=hflag[:, 0:1]),
                            reads=[b_uext[c], b_msk], writes=[b_uext[c]])
                    add("dve", lambda e, c=c, pi=pi, ri=ri: e.scalar_tensor_tensor(
                        out=sv(uext[:, c, :nsq * WA])[:, :, KA - 1:WA], in0=sv(psb[pi][:, :nt]), scalar=t_bmain[:, l, BI_VAL + c:BI_VAL + c + 1],
                        in1=sv(rotf[ri][:, :nt]), op0=ALU.add, op1=ALU.mult),
                        reads=[b_ps[pi], b_par, b_rotf[ri]], writes=[b_uext[c]])

                yield from proj_group_g(C_VAL + g0 * 128, 4, cons_val2, extra=(xh_b[:, :, 0:KA - 1] if halo_a else None))

            stage(2)
            pend_st = []
            for c in range(8):
                di = nxt("dA", 2)
                def build_dA(e, di=di, c=c):
                    ins = None
                    for k in range(KA):
                        ins = e.tensor_scalar_mul(out=dA[di][:, k, :], in0=identb[:], scalar1=t_caw[:, l, c, k:k + 1])
                    return ins

                add("dve", build_dA, reads=[b_const, b_par], writes=[b_dA[di]])
                pi = 5 if interleave else pbank()
                for k in range(KA):
                    mm(sv(psb[pi][:, :nt]), dA[di][:, k, :], sv(uext[:, c, :nsq * WA])[:, :, k:k + tl], k == 0, k == KA - 1,
                       reads=[b_dA[di], b_uext[c]], writes=[b_ps[pi]])
                    if interleave and k % 8 == 7:
                        yield
                if pend_st:
                    pend_st.pop()()
                ri = nxt("rf", 4)
                add("act", lambda e, c=c, pi=pi: e.activation(out=Ureg[:, c, :nt], in_=psb[pi][:, :nt], func=AF.Identity,
                                                              bias=t_cab[:, l, c:c + 1]),
                    reads=[b_ps[pi], b_par], writes=[b_U[c]])
                add("act", lambda e, c=c, pi=pi, ri=ri: e.activation(out=rotf[ri][:, :nt], in_=psb[pi][:, :nt], func=AF.Square,
                                                                       bias=t_cab[:, l, c:c + 1]),
                    reads=[b_ps[pi], b_par], writes=[b_rotf[ri]])
                def stats_a(c=c, ri=ri):
                    mm(psb[6][:, :nt], onesf[:], Ureg[:, c, :nt], c == 0, c == 7, reads=[b_const, b_U[c]], writes=[b_ps[6]])
                    mm(psb[7][:, :nt], onesf[:], rotf[ri][:, :nt], c == 0, c == 7, reads=[b_const, b_rotf[ri]], writes=[b_ps[7]])

                pend_st.append(stats_a)
                yield
            while pend_st:
                pend_st.pop()()
            if last and ca_out_ap is not None:
                pi = pbank()
                pv = psb[pi][:].bitcast(BF16)
                nh = nsq * (KA - 1)
                for c in range(8):
                    add("dve", lambda e, c=c: e.tensor_copy(out=sv(htmp[:, :nh]), in_=sv(uext[:, c, :nsq * WA])[:, :, tl:tl + KA - 1]),
                        reads=[b_uext[c]], writes=[b_htmp])
                    tr(pv[:nh, c * 128:(c + 1) * 128], htmp[:, :nh], identb[:],
                       reads=[b_htmp, b_const], writes=[b_ps[pi]])
                for hf in range(2):
                    ri = nxt("rf", 4)
                    add("act", lambda e, pv=pv, ri=ri, hf=hf: e.activation(out=rotf[ri][:nh, :512], in_=pv[:nh, hf * 512:(hf + 1) * 512], func=AF.Copy),
                        reads=[b_ps[pi]], writes=[b_rotf[ri]])
                    add("sp", lambda e, ri=ri, hf=hf: e.dma_start(out=ca_out_ap[:, hf * 512:(hf + 1) * 512], in_=rotf[ri][:nh, :512]),
                        reads=[b_rotf[ri]], writes=[b_out], dma=True)
            if not last:
                add("dve", lambda e: e.tensor_copy(out=uext[:, :, 0:KA - 1], in_=uext[:, :, nt:nt + KA - 1]),
                    reads=b_uext, writes=b_uext)
            ln_stats(6, 7, DA, nt)
            yield

            stage(3)
            def cons_gate(j, pi):
                add("act", lambda e, j=j, pi=pi: e.activation(out=gA[:, j, :nt], in_=psb[pi][:, :nt], func=AF.Silu,
                                                              bias=t_bmain[:, l, BI_GATE + j:BI_GATE + j + 1]),
                    reads=[b_ps[pi], b_par], writes=[b_gA[j]])

            yield from proj_group_g(C_GATE, 8, cons_gate)
            for c in range(8):
                ri = nxt("rf", 4)
                rb = nxt("rb", 2)
                add("dve", lambda e, c=c, ri=ri: e.tensor_tensor(out=rotf[ri][:, :nt], in0=Ureg[:, c, :nt], in1=st_rstd[:, :nt], op=ALU.mult),
                    reads=[b_U[c], b_rstd], writes=[b_rotf[ri]])
                add("dve", lambda e, ri=ri: e.tensor_tensor(out=rotf[ri][:, :nt], in0=rotf[ri][:, :nt], in1=st_nmr[:, :nt], op=ALU.add),
                    reads=[b_rotf[ri], b_nmr], writes=[b_rotf[ri]])
                add("act", lambda e, c=c, ri=ri, rb=rb: e.activation(out=rotb[rb][:, :nt], in_=rotf[ri][:, :nt], func=AF.Silu,
                                                                       bias=t_nab[:, l, c:c + 1], scale=t_nag[:, l, c:c + 1]),
                    reads=[b_rotf[ri], b_par], writes=[b_rotb[rb]])
                add("dve", lambda e, c=c, rb=rb: e.tensor_tensor(out=ua[:, c, :nt], in0=rotb[rb][:, :nt], in1=gA[:, c, :nt], op=ALU.mult),
                    reads=[b_rotb[rb], b_gA[c]], writes=[b_ua[c]])
                yield
            for g0 in (0, 4):
                sg_i = {}

                def cons_ga(j, pi, g0=g0):
                    ri = nxt("rf", 4)
                    sg_i[j] = ri
                    add("act", lambda e, j=j, pi=pi, ri=ri: e.activation(out=rotf[ri][:, :nt], in_=psb[pi][:, :nt], func=AF.Sigmoid,
                                                                           bias=t_bmain[:, l, BI_GA + g0 + j:BI_GA + g0 + j + 1]),
                        reads=[b_ps[pi], b_par], writes=[b_rotf[ri]])

                yield from proj_group_g(C_GA + g0 * 128, 4, cons_ga)
                wi, wv = wmat(w_a_out[l, :, g0 * 128:(g0 + 4) * 128], 8, 512)
                for j in range(4):
                    oc = g0 + j
                    pi = pbank()
                    for k in range(8):
                        mm(psb[pi][:, :nt], wv[:, k, j * 128:(j + 1) * 128], ua[:, k, :nt], k == 0, k == 7,
                           reads=b_wbuf[wi] + [b_ua[k]], writes=[b_ps[pi]])
                    ri = sg_i[j]
                    add("dve", lambda e, oc=oc, pi=pi, ri=ri: e.tensor_tensor(out=m1[:, oc, :nt], in0=psb[pi][:, :nt], in1=rotf[ri][:, :nt], op=ALU.mult),
                        reads=[b_ps[pi], b_rotf[ri]], writes=[b_m1[oc]])
                    yield

        interleave = (mode == "p2" and nsq == 1 and INTERLEAVE)
        ga = gen_branch_a() if do_a else iter(())

        def pump(n=1):
            for _ in range(n):
                try:
                    next(ga)
                except StopIteration:
                    return

        def drain():
            for _ in ga:
                pass

        if not interleave:
            drain()
        stage(4)
        if do_a:
            def cons_z(j, pi):
                add("act", lambda e, j=j, pi=pi: e.activation(out=Zreg[:, j, :nt], in_=psb[pi][:, :nt], func=AF.Silu,
                                                              bias=t_bmain[:, l, BI_Z + j:BI_Z + j + 1]),
                    reads=[b_ps[pi], b_par], writes=[b_Z[j]])

            proj_group(C_Z, 16, cons_z)
        else:
            pass

        if do_xbc:
            halo_b = first and seg
            if first and not seg:
                if cb_in_ap is None:
                    add("dve", lambda e: e.memset(hB[:, :, :KB - 1], 0.0), writes=b_hB)
                else:
                    nhb = nsq * (KB - 1)
                    add("sp", lambda e: e.dma_start(out=Uflat[:nhb, :DXBC], in_=cb_in_ap), writes=b_U, dma=True)
                    pi = pbank()
                    for c in range(24):
                        tr(psb[pi][:, c * 16:c * 16 + nhb], Uflat[:nhb, c * 128:(c + 1) * 128], identf[:nhb, :nhb],
                           reads=b_U + [b_const], writes=[b_ps[pi]])
                    add("act", lambda e, pi=pi: e.activation(out=hB[:, :, :nhb], in_=psb[pi][:, :384].rearrange("p (c t) -> p c t", c=24)[:, :, 0:nhb],
                                                             func=AF.Copy),
                        reads=[b_ps[pi]], writes=b_hB)

            pend_xbc = []

            def cons_xbc(j, pi, pih=None):
                if pend_xbc:
                    pend_xbc.pop()()
                xi = nxt("xp", 2)
                di = nxt("dB", 2)
                if pih is not None:
                    add("dve", lambda e, j=j, pih=pih: e.scalar_tensor_tensor(
                        out=hB[:, j, :KB - 1], in0=psb[pih][:, :KB - 1], scalar=t_bmain[:, l, BI_XBC + j:BI_XBC + j + 1],
                        in1=hflag[:, 0:KB - 1], op0=ALU.add, op1=ALU.mult),
                        reads=[b_ps[pih], b_par, b_msk], writes=[b_hB[j]])
                add("dve", lambda e, xi=xi, j=j: e.tensor_copy(out=sv(xpb[xi][:, :nsq * WB])[:, :, 0:KB - 1], in_=sv(hB[:, j, :nsq * (KB - 1)])),
                    reads=[b_hB[j]], writes=[b_xpb[xi]])
                add("act", lambda e, xi=xi, j=j, pi=pi: e.activation(out=sv(xpb[xi][:, :nsq * WB])[:, :, KB - 1:WB], in_=sv(psb[pi][:, :nt]), func=AF.Identity,
                                                                       bias=t_bmain[:, l, BI_XBC + j:BI_XBC + j + 1]),
                    reads=[b_ps[pi], b_par], writes=[b_xpb[xi]])
                def build_dB(e, di=di, j=j):
                    ins = None
                    for k in range(KB):
                        ins = e.tensor_scalar_mul(out=dB[di][:, k, :], in0=identb[:], scalar1=t_cbw[:, l, j, k:k + 1])
                    return ins

                add("dve", build_dB, reads=[b_const, b_par], writes=[b_dB[di]])

                def part2(j=j, xi=xi, di=di):
                    p2 = pbank()
                    for k in range(KB):
                        mm(sv(psb[p2][:, :nt]), dB[di][:, k, :], sv(xpb[xi][:, :nsq * WB])[:, :, k:k + tl], k == 0, k == KB - 1,
                           reads=[b_dB[di], b_xpb[xi]], writes=[b_ps[p2]])
                    add("act", lambda e, j=j, p2=p2: e.activation(out=xc[:, j, :nt], in_=psb[p2][:, :nt], func=AF.Silu,
                                                                  bias=t_cbb[:, l, j:j + 1]),
                        reads=[b_ps[p2], b_par], writes=[b_xc[j]])
                    add("dve", lambda e, xi=xi, j=j: e.tensor_copy(out=sv(hB[:, j, :nsq * (KB - 1)]), in_=sv(xpb[xi][:, :nsq * WB])[:, :, tl:tl + KB - 1]),
                        reads=[b_xpb[xi]], writes=[b_hB[j]])

                pend_xbc.append(part2)

            proj_group(C_XBC, 24, cons_xbc, extra=(xh_b[:, :, KA - KB:KA - 1] if halo_b else None))
            while pend_xbc:
                pend_xbc.pop()()
            if xcs_ap is not None:
                add("sp", lambda e: e.dma_start(out=xcs_ap, in_=xc[:, :, :nt]), reads=b_xc, writes=[xsc_buf(("xcs",) + tuple(xsc_key))], dma=True)
            if last and cb_out_ap is not None:
                for h0 in range(0, 24, 8):
                    pi = pbank()
                    pv = psb[pi][:].bitcast(BF16)
                    nhb = nsq * (KB - 1)
                    for c in range(8):
                        tr(pv[:nhb, c * 128:(c + 1) * 128], hB[:, h0 + c, :nhb], identb[:],
                           reads=[b_hB[h0 + c], b_const], writes=[b_ps[pi]])
                    add("act", lambda e, pv=pv, h0=h0, nhb=nhb: e.activation(out=Uflat[:nhb, h0 * 128:(h0 + 8) * 128], in_=pv[:nhb, :1024], func=AF.Copy),
                        reads=[b_ps[pi]], writes=b_U)
                add("sp", lambda e: e.dma_start(out=cb_out_ap, in_=Uflat[:nsq * (KB - 1), :DXBC]), reads=b_U, writes=[b_out], dma=True)

        else:
            add("sp", lambda e: e.dma_start(out=xc[:, :, :nt], in_=xcs_ap), reads=[xsc_buf(("xcs",) + tuple(xsc_key))], writes=b_xc, dma=True)
        stage(5)
        add("pool", lambda e: e.dma_start(out=wdt[:], in_=w_in[l, :, C_DT:C_DT + NH].rearrange("(k p) n -> p k n", p=128)),
            writes=[b_wdt], dma=True)
        pi = pbank()
        for b in range(nb):
            for k in range(8):
                mm(psb[pi][:qb, b * NH:(b + 1) * NH], xb[:, k, b * qb:(b + 1) * qb], wdt[:, k, :], k == 0, k == 7,
                   reads=[b_xb[k], b_wdt], writes=[b_ps[pi]])
        add("dve", lambda e, pi=pi: e.tensor_tensor(out=dtv[:qb, :nb, :], in0=psb[pi][:qb, :nb * NH].rearrange("p (b h) -> p b h", b=nb),
                                                    in1=t_dtb[:qb, l:l + 1, :].to_broadcast([qb, nb, NH]), op=ALU.add),
            reads=[b_ps[pi], b_par], writes=[b_dtv])
        add("act", lambda e: e.activation(out=dtv[:qb, :nb, :], in_=dtv[:qb, :nb, :], func=AF.Exp), reads=[b_dtv], writes=[b_dtv])
        add("act", lambda e: e.activation(out=dtT[:qb, :nb, :], in_=dtv[:qb, :nb, :], func=AF.Ln, bias=1.0, scale=1.0),
            reads=[b_dtv], writes=[b_dtT])
        add("dve", lambda e: e.tensor_tensor(out=aT[:qb, :nb, :], in0=dtT[:qb, :nb, :],
                                             in1=t_A[:qb, l:l + 1, :].to_broadcast([qb, nb, NH]), op=ALU.mult),
            reads=[b_dtT, b_par], writes=[b_aT])

        stage(6)
        if first and do_a:
            def build_ddsk(e):
                ins = None
                for c in range(16):
                    ins = e.tensor_scalar_mul(out=ddsk[:, c, :], in0=identb[:], scalar1=t_dsk[:, l, c:c + 1])
                return ins

            add("dve", build_ddsk, reads=[b_const, b_par], writes=[b_ddsk])
        if first:
            if mode == "p1":
                add("dve", lambda e: e.memset(Sn[:], 0.0), writes=[b_Sn])
                add("dve", lambda e: e.memset(dtot[:], 1.0), writes=[b_dtot])
            elif mode == "full":
                if st_in_ap is None:
                    add("dve", lambda e: e.memset(Sn[:], 0.0), writes=[b_Sn])
                elif nsq == 1:
                    add("sp", lambda e: e.dma_start(out=Sn[:], in_=st_in_ap.rearrange("(c p) n -> p c n", p=128)), writes=[b_Sn], dma=True)

        if interleave:
            WBN[0] = 2
            PBN[0] = 5
        for b in range(nb):
            sl = slice(b * qb, (b + 1) * qb)
            if interleave:
                pump()
            pi = pbank()
            mm(psb[pi][:qb, :NH], Lm[:qb, :qb], aT[:qb, b, :], True, True, reads=[b_const, b_aT], writes=[b_ps[pi]])
            add("act", lambda e, pi=pi: e.activation(out=d2e[:qb, :], in_=psb[pi][:qb, :NH], func=AF.Exp), reads=[b_ps[pi]], writes=[b_d2e])
            add("dve", lambda e, b=b: e.tensor_copy(out=abc[:qb, :].rearrange("p (h q) -> p h q", q=HP),
                                                     in_=aT[:qb, b, :].unsqueeze(2).to_broadcast([qb, NH, HP])),
                reads=[b_aT], writes=[b_abc])
            if not state_only and nsq == 1:
                add("act", lambda e: e.activation(out=Snb, in_=Sn[:], func=AF.Copy), reads=[b_Sn], writes=[b_Snb])
                for half in range(2):
                    pi = pbank()
                    pv = psb[pi][:].bitcast(BF16)
                    for cc in range(8):
                        c = half * 8 + cc
                        tr(pv[:, cc * 128:(cc + 1) * 128], Snb[:, c, :], identb[:], reads=[b_Snb, b_const], writes=[b_ps[pi]])
                    add("dve", lambda e, pv=pv, half=half: e.tensor_copy(out=Sb_[:, half * 1024:(half + 1) * 1024], in_=pv[:, :1024]),
                        reads=[b_ps[pi]], writes=[b_Sb])
            if interleave:
                pump()
            pi = pbank()
            pv = psb[pi][:].bitcast(BF16)
            for g in range(NG):
                tr(pv[:qb, g * 128:(g + 1) * 128], xc[:, 16 + g, sl], identb[:], reads=[b_xc[16 + g], b_const], writes=[b_ps[pi]])
            add("act", lambda e, pv=pv: e.activation(out=BT[:qb, :, :], in_=pv[:qb, :512].rearrange("p (g n) -> p g n", g=NG), func=AF.Copy),
                reads=[b_ps[pi]], writes=[b_BT])
            for half in range(2):
                pi = pbank()
                pv = psb[pi][:].bitcast(BF16)
                for cc in range(8):
                    c = half * 8 + cc
                    tr(pv[:qb, cc * 128:(cc + 1) * 128], xc[:, c, sl], identb[:], reads=[b_xc[c], b_const], writes=[b_ps[pi]])
                add("dve", lambda e, pv=pv, half=half, b=b: e.tensor_tensor(
                    out=xdt[:qb, half * 1024:(half + 1) * 1024].rearrange("p (h q) -> p h q", q=HP),
                    in0=pv[:qb, :1024].rearrange("p (h q) -> p h q", q=HP),
                    in1=dtT[:qb, b, half * 16:(half + 1) * 16].unsqueeze(2).to_broadcast([qb, 16, HP]), op=ALU.mult),
                    reads=[b_ps[pi], b_dtT], writes=[b_xdt])
            if interleave:
                pump()
            add("dve", lambda e: e.tensor_tensor(out=xdtd[:qb, :].rearrange("p (h q) -> p h q", q=HP),
                                                  in0=xdt[:qb, :].rearrange("p (h q) -> p h q", q=HP),
                                                  in1=d2e[:qb, :].unsqueeze(2).to_broadcast([qb, NH, HP]), op=ALU.mult),
                reads=[b_xdt, b_d2e], writes=[b_xdtd])
            if nsq > 1:
                PBN[0] = 4
                pi = pbank()
                for c in range(16):
                    mm(psb[pi][:, c * 4:c * 4 + nsq], abc[:qb, c * 128:(c + 1) * 128], sqm[:qb, 0:nsq], True, True,
                       reads=[b_abc, b_const], writes=[b_ps[pi]])
                add("act", lambda e, pi=pi: e.activation(out=dec4[:, :, :nsq], in_=psb[pi][:, :64].rearrange("p (c s) -> p c s", s=4)[:, :, :nsq], func=AF.Exp),
                    reads=[b_ps[pi]], writes=[b_dec4])
                for i in range(nsq):
                    cs = slice(i * tl, (i + 1) * tl)
                    add("sp", lambda e, i=i: e.dma_start(out=Sn[:], in_=st_in_ap[i].rearrange("(c p) n -> p c n", p=128)), writes=[b_Sn], dma=True)
                    Snb2 = rhsA[0][:].rearrange("p h n -> p (h n)").bitcast(BF16).rearrange("p (c n) -> p c n", c=16)
                    add("act", lambda e, Snb2=Snb2: e.activation(out=Snb2, in_=Sn[:], func=AF.Copy), reads=[b_Sn], writes=[b_rhsA[0]])
                    for half in range(2):
                        pi = pbank()
                        pv = psb[pi][:].bitcast(BF16)
                        for cc in range(8):
                            c = half * 8 + cc
                            tr(pv[:, cc * 128:(cc + 1) * 128], Snb2[:, c, :], identb[:], reads=[b_rhsA[0], b_const], writes=[b_ps[pi]])
                        add("dve", lambda e, pv=pv, half=half: e.tensor_copy(out=Sb_[:, half * 1024:(half + 1) * 1024], in_=pv[:, :1024]),
                            reads=[b_ps[pi]], writes=[b_Sb])
                    for g in range(NG):
                        for jj in range(4):
                            c = 4 * g + jj
                            mm(psb[4 + g][:, jj * 128 + i * tl:jj * 128 + (i + 1) * tl], Sb_[:, c * 128:(c + 1) * 128], xc[:, 20 + g, cs], True, True,
                               reads=[b_Sb, b_xc[20 + g]], writes=[b_ps[4 + g]])
                    add("dve", lambda e, i=i: e.tensor_scalar_mul(out=BTm[:qb, :, :], in0=BT[:qb, :, :], scalar1=sqm[:qb, i:i + 1]),
                        reads=[b_BT, b_const], writes=[b_BTm])
                    add("dve", lambda e, i=i: e.tensor_tensor(out=Sn[:], in0=Sn[:], in1=dec4[:, :, i:i + 1].to_broadcast([128, 16, 128]), op=ALU.mult),
                        reads=[b_Sn, b_dec4], writes=[b_Sn])
                    for g in range(NG):
                        pi = pbank()
                        for jj in range(4):
                            c = 4 * g + jj
                            mm(psb[pi][:, jj * 128:(jj + 1) * 128], xdtd[:qb, c * 128:(c + 1) * 128], BTm[:qb, g, :], True, True,
                               reads=[b_xdtd, b_BTm], writes=[b_ps[pi]])
                        add("dve", lambda e, g=g, pi=pi: e.tensor_tensor(out=Sn[:, 4 * g:4 * g + 4, :], in0=Sn[:, 4 * g:4 * g + 4, :],
                                                                          in1=psb[pi][:, :].rearrange("p (j n) -> p j n", j=4), op=ALU.add),
                            reads=[b_Sn, b_ps[pi]], writes=[b_Sn])
                    add("sp", lambda e, i=i: e.dma_start(out=st_out_ap[i].rearrange("(c p) n -> p c n", p=128), in_=Sn[:]),
                        reads=[b_Sn], writes=[b_out], dma=True)
            if not state_only:
                pi = pbank()
                for g in range(NG):
                    mm(psb[pi][:qb, g * 128:g * 128 + qb], xc[:, 16 + g, sl], xc[:, 20 + g, sl], True, True,
                       reads=[b_xc[16 + g], b_xc[20 + g]], writes=[b_ps[pi]])
                add("dve", lambda e, pi=pi: e.tensor_tensor(out=CBm[:qb, :, :qb],
                                                            in0=psb[pi][:qb, :].rearrange("p (g n) -> p g n", g=NG)[:, :, :qb],
                                                            in1=Um[:qb, :qb].unsqueeze(1).to_broadcast([qb, NG, qb]), op=ALU.mult),
                    reads=[b_ps[pi], b_const], writes=[b_CBm])
                if interleave:
                    pump()
                for g in range(NG):
                    gi = nxt("g", 2)
                    add("dve", lambda e, gi=gi, g=g, b=b: e.tensor_tensor(
                        out=rhsA[gi][:qb, :, :qb],
                        in0=aT[:qb, b, 8 * g:8 * g + 8].unsqueeze(2).to_broadcast([qb, 8, qb]),
                        in1=Um[:qb, :qb].unsqueeze(1).to_broadcast([qb, 8, qb]), op=ALU.mult),
                        reads=[b_aT, b_const], writes=[b_rhsA[gi]])
                    hper = 512 // qb if qb < 128 else 4
                    hper = min(8, hper)
                    for h0 in range(0, 8, hper):
                        pi = pbank()
                        mm(psb[pi][:qb, :hper * qb].rearrange("p (h n) -> p h n", h=hper), Lm[:qb, :qb], rhsA[gi][:qb, h0:h0 + hper, :qb],
                           True, True, reads=[b_const, b_rhsA[gi]], writes=[b_ps[pi]])
                        add("act", lambda e, pi=pi, gi=gi, h0=h0, hper=hper: e.activation(
                            out=eseg[gi][:qb, h0:h0 + hper, :qb], in_=psb[pi][:qb, :hper * qb].rearrange("p (h n) -> p h n", h=hper), func=AF.Exp),
                            reads=[b_ps[pi]], writes=[b_eseg[gi]])
                        if interleave:
                            pump()
                    add("dve", lambda e, gi=gi, g=g: e.tensor_tensor(out=MT[gi][:qb, :, :qb], in0=eseg[gi][:qb, :, :qb],
                                                                      in1=CBm[:qb, g:g + 1, :qb].to_broadcast([qb, 8, qb]), op=ALU.mult),
                        reads=[b_eseg[gi], b_CBm], writes=[b_MT[gi]])
                    pyd = pbank()
                    for jj in range(4):
                        c = 4 * g + jj
                        mm(psb[pyd][:, jj * 128:jj * 128 + qb], ddsk[:, c, :], xc[:, c, sl], True, False,
                           reads=[b_ddsk, b_xc[c]], writes=[b_ps[pyd]])
                        for hh in range(2):
                            h = 8 * g + 2 * jj + hh
                            mm(psb[pyd][64 * hh:64 * hh + 64, jj * 128:jj * 128 + qb], xdt[:qb, h * HP:(h + 1) * HP], MT[gi][:qb, 2 * jj + hh, :qb],
                               False, True, reads=[b_xdt, b_MT[gi]], writes=[b_ps[pyd]])
                    if nsq > 1:
                        pyo = 4 + g
                    else:
                        pyo = pbank()
                        for jj in range(4):
                            c = 4 * g + jj
                            mm(psb[pyo][:, jj * 128:jj * 128 + qb], Sb_[:, c * 128:(c + 1) * 128], xc[:, 20 + g, sl], True, True,
                               reads=[b_Sb, b_xc[20 + g]], writes=[b_ps[pyo]])
                    pE = pbank()
                    for jj in range(4):
                        c = 4 * g + jj
                        mm(psb[pE][:, jj * 128:jj * 128 + qb], abc[:qb, c * 128:(c + 1) * 128], Um[:qb, :qb], True, True,
                           reads=[b_abc, b_const], writes=[b_ps[pE]])
                    ei = nxt("e", 2)

                    def v3(t):
                        return t.rearrange("p (j n) -> p j n", j=4)[:, :, :qb]

                    add("act", lambda e, ei=ei, pE=pE: e.activation(out=Et[ei][:, :, :qb], in_=v3(psb[pE][:, :]), func=AF.Exp),
                        reads=[b_ps[pE]], writes=[b_Et[ei]])
                    add("dve", lambda e, ei=ei, pyo=pyo: e.tensor_tensor(out=t1[ei][:, :, :qb], in0=v3(psb[pyo][:, :]), in1=Et[ei][:, :, :qb], op=ALU.mult),
                        reads=[b_ps[pyo], b_Et[ei]], writes=[b_t1[ei]])
                    add("dve", lambda e, ei=ei, pyd=pyd: e.tensor_tensor(out=t1[ei][:, :, :qb], in0=v3(psb[pyd][:, :]), in1=t1[ei][:, :, :qb], op=ALU.add),
                        reads=[b_ps[pyd], b_t1[ei]], writes=[b_t1[ei]])
                    add("dve", lambda e, ei=ei, g=g, sl=sl: e.tensor_tensor(out=Zreg[:, 4 * g:4 * g + 4, sl], in0=t1[ei][:, :, :qb],
                                                                             in1=Zreg[:, 4 * g:4 * g + 4, sl], op=ALU.mult),
                        reads=[b_t1[ei]] + b_Z[4 * g:4 * g + 4], writes=b_Z[4 * g:4 * g + 4])
                    if interleave:
                        pump(2)
            if nsq > 1:
                PBN[0] = 6
                continue
            pi = pbank()
            for c in range(16):
                mm(psb[pi][:, c:c + 1], abc[:qb, c * 128:(c + 1) * 128], onesf[:qb, 0:1], True, True,
                   reads=[b_abc, b_const], writes=[b_ps[pi]])
            add("act", lambda e, pi=pi: e.activation(out=dec[:, :], in_=psb[pi][:, :16], func=AF.Exp), reads=[b_ps[pi]], writes=[b_dec])
            if state_only:
                add("dve", lambda e: e.tensor_tensor(out=dtot[:], in0=dtot[:], in1=dec[:, :], op=ALU.mult), reads=[b_dtot, b_dec], writes=[b_dtot])
            add("dve", lambda e: e.tensor_tensor(out=Sn[:], in0=Sn[:], in1=dec[:, :].unsqueeze(2).to_broadcast([128, 16, 128]), op=ALU.mult),
                reads=[b_Sn, b_dec], writes=[b_Sn])
            for g in range(NG):
                pi = pbank()
                for jj in range(4):
                    c = 4 * g + jj
                    mm(psb[pi][:, jj * 128:(jj + 1) * 128], xdtd[:qb, c * 128:(c + 1) * 128], BT[:qb, g, :], True, True,
                       reads=[b_xdtd, b_BT], writes=[b_ps[pi]])
                add("dve", lambda e, g=g, pi=pi: e.tensor_tensor(out=Sn[:, 4 * g:4 * g + 4, :], in0=Sn[:, 4 * g:4 * g + 4, :],
                                                                  in1=psb[pi][:, :].rearrange("p (j n) -> p j n", j=4), op=ALU.add),
                    reads=[b_Sn, b_ps[pi]], writes=[b_Sn])
        if interleave:
            WBN[0] = NWB
            drain()
            PBN[0] = 6
            if next_x is not None and PREFETCH:
                nx_ap, nx_key = next_x
                add("pool", lambda e: e.dma_start(out=gA[:, :, :nt], in_=nx_ap), reads=[xsc_buf(nx_key)], writes=b_gA, dma=True)
                PREF[0] = True
                PAR[0] ^= 1
        if last and st_out_ap is not None and do_a and nsq == 1:
            add("sp", lambda e: e.dma_start(out=st_out_ap.rearrange("(c p) n -> p c n", p=128), in_=Sn[:]), reads=[b_Sn], writes=[b_out], dma=True)

        if not do_a:
            return
        if True:
            stage(7)
            for g in range(NG):
                pi = pbank()
                pend_r = []
                for jj in range(4):
                    c = 4 * g + jj
                    rb = nxt("rb", 2)
                    add("act", lambda e, c=c, rb=rb: e.activation(out=rotb[rb][:, :nt], in_=Zreg[:, c, :nt], func=AF.Square),
                        reads=[b_Z[c]], writes=[b_rotb[rb]])
                    if pend_r:
                        pend_r.pop()()

                    def mm_r(jj=jj, rb=rb, pi=pi):
                        mm(psb[pi][:, :nt], onesb[:], rotb[rb][:, :nt], jj == 0, jj == 3, reads=[b_const, b_rotb[rb]], writes=[b_ps[pi]])

                    pend_r.append(mm_r)
                while pend_r:
                    pend_r.pop()()
                ri = nxt("rf", 4)
                add("act", lambda e, pi=pi, ri=ri: e.activation(out=rotf[ri][:, :nt], in_=psb[pi][:, :nt], func=AF.Ln, bias=EPS, scale=1.0 / 512),
                    reads=[b_ps[pi]], writes=[b_rotf[ri]])
                add("act", lambda e, ri=ri: e.activation(out=rotf[ri][:, :nt], in_=rotf[ri][:, :nt], func=AF.Exp, scale=-0.5),
                    reads=[b_rotf[ri]], writes=[b_rotf[ri]])
                for jj in range(4):
                    c = 4 * g + jj
                    add("dve", lambda e, c=c, ri=ri: e.scalar_tensor_tensor(out=Zreg[:, c, :nt], in0=Zreg[:, c, :nt], scalar=t_gnw[:, l, c:c + 1],
                                                                            in1=rotf[ri][:, :nt], op0=ALU.mult, op1=ALU.mult),
                        reads=[b_Z[c], b_par, b_rotf[ri]], writes=[b_Z[c]])
            for g0 in (0, 4):
                sg_i = {}

                def cons_gb(j, pi, g0=g0):
                    ri = nxt("rf", 4)
                    sg_i[j] = ri
                    add("act", lambda e, j=j, pi=pi, ri=ri: e.activation(out=rotf[ri][:, :nt], in_=psb[pi][:, :nt], func=AF.Sigmoid,
                                                                           bias=t_bmain[:, l, BI_GB + g0 + j:BI_GB + g0 + j + 1]),
                        reads=[b_ps[pi], b_par], writes=[b_rotf[ri]])

                proj_group(C_GB + g0 * 128, 4, cons_gb)
                for h2 in range(2):
                    wi, wv = wmat(w_b_out[l, :, (g0 + 2 * h2) * 128:(g0 + 2 * h2 + 2) * 128], 16, 256)
                    for j2 in range(2):
                        j = 2 * h2 + j2
                        oc = g0 + j
                        pi = pbank()
                        for k in range(16):
                            mm(psb[pi][:, :nt], wv[:, k, j2 * 128:(j2 + 1) * 128], Zreg[:, k, :nt], k == 0, k == 15,
                               reads=b_wbuf[wi] + [b_Z[k]], writes=[b_ps[pi]])
                        ri = sg_i[j]
                        add("dve", lambda e, pi=pi, ri=ri: e.tensor_tensor(out=rotf[ri][:, :nt], in0=psb[pi][:, :nt], in1=rotf[ri][:, :nt], op=ALU.mult),
                            reads=[b_ps[pi], b_rotf[ri]], writes=[b_rotf[ri]])
                        add("dve", lambda e, oc=oc, ri=ri: e.tensor_tensor(out=m1[:, oc, :nt], in0=m1[:, oc, :nt], in1=rotf[ri][:, :nt], op=ALU.add),
                            reads=[b_m1[oc], b_rotf[ri]], writes=[b_m1[oc]])

            stage(8)
            if do_a:
                ptok2 = rhsA[0][:].rearrange("p h n -> p (h n)").rearrange("p (b d) -> p b d", d=PLE)
                add("sp", lambda e: e.dma_start(out=ptok2[:qb, :nb, :], in_=p_tok_ap.rearrange("(b p) d -> p b d", p=qb)),
                    writes=[b_rhsA[0]], dma=True)
                for c in range(2):
                    pi = pbank()
                    for b in range(nb):
                        tr(psb[pi][:, b * qb:(b + 1) * qb], ptok2[:qb, b, c * 128:(c + 1) * 128], identf[:qb, :qb],
                           reads=[b_rhsA[0], b_const], writes=[b_ps[pi]])
                    add("act", lambda e, c=c, pi=pi: e.activation(out=pT[:, c, :nt], in_=psb[pi][:, :nt], func=AF.Copy),
                        reads=[b_ps[pi]], writes=[b_pT])

            add("sp", lambda e: e.dma_start(out=Ureg[:, :, :nt], in_=xsc_ap), reads=[bx], writes=b_U, dma=True)
            for g0 in (0, 4):
                sg_i = {}

                def cons_gp(j, pi, g0=g0):
                    ri = nxt("rf", 4)
                    sg_i[j] = ri
                    add("act", lambda e, j=j, pi=pi, ri=ri: e.activation(out=rotf[ri][:, :nt], in_=psb[pi][:, :nt], func=AF.Sigmoid,
                                                                           bias=t_bmain[:, l, BI_GP + g0 + j:BI_GP + g0 + j + 1]),
                        reads=[b_ps[pi], b_par], writes=[b_rotf[ri]])

                proj_group(C_GP + g0 * 128, 4, cons_gp)
                wi, wv = wmat(w_out[l, :, g0 * 128:(g0 + 4) * 128], 8, 512)
                wpi, wpv = wmat(w_ple[l, :, g0 * 128:(g0 + 4) * 128], 2, 512)
                pend_o = []
                for j in range(4):
                    oc = g0 + j
                    pp = pbank()
                    for k in range(2):
                        mm(psb[pp][:, :nt], wpv[:, k, j * 128:(j + 1) * 128], pT[:, k, :nt], k == 0, k == 1,
                           reads=b_wbuf[wpi] + [b_pT], writes=[b_ps[pp]])
                    pm = pbank()
                    for k in range(8):
                        mm(psb[pm][:, :nt], wv[:, k, j * 128:(j + 1) * 128], m1[:, k, :nt], k == 0, k == 7,
                           reads=b_wbuf[wi] + [b_m1[k]], writes=[b_ps[pm]])
                    if pend_o:
                        pend_o.pop()()
                    ri = sg_i[j]
                    add("dve", lambda e, pp=pp, ri=ri: e.tensor_tensor(out=rotf[ri][:, :nt], in0=psb[pp][:, :nt], in1=rotf[ri][:, :nt], op=ALU.mult),
                        reads=[b_ps[pp], b_rotf[ri]], writes=[b_rotf[ri]])
                    add("dve", lambda e, pm=pm, ri=ri: e.tensor_tensor(out=rotf[ri][:, :nt], in0=psb[pm][:, :nt], in1=rotf[ri][:, :nt], op=ALU.add),
                        reads=[b_ps[pm], b_rotf[ri]], writes=[b_rotf[ri]])
                    add("dve", lambda e, oc=oc, ri=ri: e.scalar_tensor_tensor(out=Ureg[:, oc, :nt], in0=Ureg[:, oc, :nt], scalar=ALPHA, in1=rotf[ri][:, :nt],
                                                                              op0=ALU.mult, op1=ALU.add),
                        reads=[b_U[oc], b_rotf[ri]], writes=[b_U[oc]])
                    r2 = nxt("rf", 4)
                    add("act", lambda e, oc=oc, r2=r2: e.activation(out=rotf[r2][:, :nt], in_=Ureg[:, oc, :nt], func=AF.Square),
                        reads=[b_U[oc]], writes=[b_rotf[r2]])
                    def stats_o(oc=oc, r2=r2):
                        mm(psb[6][:, :nt], onesf[:], Ureg[:, oc, :nt], oc == 0, oc == 7, reads=[b_const, b_U[oc]], writes=[b_ps[6]])
                        mm(psb[7][:, :nt], onesf[:], rotf[r2][:, :nt], oc == 0, oc == 7, reads=[b_const, b_rotf[r2]], writes=[b_ps[7]])

                    pend_o.append(stats_o)
                while pend_o:
                    pend_o.pop()()
            ln_stats(6, 7, D, nt)
            for c in range(8):
                ri = nxt("rf", 4)
                add("dve", lambda e, c=c, ri=ri: e.tensor_tensor(out=rotf[ri][:, :nt], in0=Ureg[:, c, :nt], in1=st_rstd[:, :nt], op=ALU.mult),
                    reads=[b_U[c], b_rstd], writes=[b_rotf[ri]])
                add("dve", lambda e, ri=ri: e.tensor_tensor(out=rotf[ri][:, :nt], in0=rotf[ri][:, :nt], in1=st_nmr[:, :nt], op=ALU.add),
                    reads=[b_rotf[ri], b_nmr], writes=[b_rotf[ri]])
                add("act", lambda e, c=c, ri=ri: e.activation(out=Ureg[:, c, :nt], in_=rotf[ri][:, :nt], func=AF.Identity,
                                                              bias=t_lnb[:, l, c:c + 1], scale=t_lng[:, l, c:c + 1]),
                    reads=[b_rotf[ri], b_par], writes=[b_U[c]])
            if l < L - 1:
                stage(0.3)
                add("sp", lambda e: e.dma_start(out=xsc_ap, in_=Ureg[:, :, :nt]), reads=b_U, writes=[bx], dma=True)
            else:
                for b in range(nb):
                    for h2 in range(2):
                        pi = pbank()
                        for cc in range(4):
                            c = h2 * 4 + cc
                            tr(psb[pi][:qb, cc * 128:(cc + 1) * 128], Ureg[:, c, b * qb:(b + 1) * qb], identf[:],
                               reads=[b_U[c], b_const], writes=[b_ps[pi]])
                        add("act", lambda e, pi=pi, b=b, h2=h2: e.activation(out=stg[:qb, b, h2 * 512:(h2 + 1) * 512], in_=psb[pi][:qb, :512], func=AF.Copy),
                            reads=[b_ps[pi]], writes=b_Z)
                add("sp", lambda e: e.dma_start(out=y_tok_ap.rearrange("(b p) d -> p b d", p=qb), in_=stg[:qb, :nb, :]),
                    reads=b_Z, writes=[b_out], dma=True)

    ntp = Tp // NT
    NHL = KA - 1
    grp = [list(g) for g in groups]

    batched = (n_samp * Ts == 128)

    def sample_tiles(l, s0, s1):
        if batched:
            if s0 != 0:
                return
            fl = lambda ap: ap.rearrange("s t d -> (s t) d")
            tile_layer(l, n_samp * Ts, n_samp * Ts, fl(xs_in), fl(ps_in[l]), xsc_s, ("s", 0), fl(ys), True, True,
                       fl(ca_in[l]), fl(cb_in[l]), st_in[l], fl(cas_o[l]), fl(cbs_o[l]), sts_o[l], nsq=n_samp)
            return
        for s in range(s0, s1):
            tile_layer(l, Ts, Ts, xs_in[s], ps_in[l, s], xsc_s[:, :, s * Ts:(s + 1) * Ts], ("s", s), ys[s], True, True,
                       ca_in[l, s], cb_in[l, s], st_in[l, s], cas_o[l, s], cbs_o[l, s], sts_o[l, s])

    def prompt_tiles(l, mode):
        for t in range(ntp):
            t0 = t * NT
            tile_layer(l, NT, 128, xp_in[t0:t0 + NT, :], pp_in[l, t0:t0 + NT, :], xsc_p[:, :, t0:t0 + NT], ("p", t),
                       yp[t0:t0 + NT, :], t == 0, t == ntp - 1,
                       None, None, None, cap_o[l], cbp_o[l], stp_o[l], mode=mode,
                       xcs_ap=(xcs[t] if seg else None), seg=seg,
                       next_x=((xsc_p[:, :, t0 + NT:t0 + 2 * NT], ("p", t + 1)) if (mode == "p2" and t + 1 < ntp) else None))

    try:
        if seg and ntp > 0:
            add("sp", lambda e: e.dma_start(out=stg[:NHL, 0, :], in_=xprev), writes=b_Z, dma=True)
            pi = pbank()
            for c in range(8):
                tr(psb[pi][:, c * 32:c * 32 + NHL], stg[:NHL, 0, c * 128:(c + 1) * 128], identf[:NHL, :NHL],
                   reads=b_Z + [b_const], writes=[b_ps[pi]])
            add("act", lambda e, pi=pi: e.activation(out=xh_b[:, :, 0:NHL],
                                                     in_=psb[pi][:, :256].rearrange("p (c t) -> p c t", c=8)[:, :, 0:NHL], func=AF.Copy),
                reads=[b_ps[pi]], writes=[b_xh])
        for l in range(L if KSTOP >= 99 else 1):
            h = n_samp // 2
            if not seg:
                if ntp > 0:
                    prompt_tiles(l, "full")
                sample_tiles(l, 0, n_samp)
                continue
            prompt_tiles(l, "p1")
            add("sp", lambda e: e.dma_start(out=st_src[:, 0:DI], in_=Sn[:].rearrange("p c n -> p (c n)")), reads=[b_Sn], writes=[b_stsrc], dma=True)
            add("sp", lambda e: e.dma_start(out=dt_src[:, :], in_=dtot[:]), reads=[b_dtot], writes=[b_dtsrc], dma=True)
            add("pool", lambda e: e.collective_compute("AllGather", ALU.bypass, replica_groups=grp, ins=[st_src], outs=[st_all]),
                reads=[b_stsrc], writes=[b_stall], cc=True)
            add("pool", lambda e: e.collective_compute("AllGather", ALU.bypass, replica_groups=grp, ins=[dt_src], outs=[dt_all]),
                reads=[b_dtsrc], writes=[b_dtall], cc=True)
            if batched:
                if l == 0:
                    sample_tiles(0, 0, n_samp)
            else:
                sample_tiles(l, 0, h)
            add("dve", lambda e: e.memset(Sn[:], 0.0), writes=[b_Sn])
            for r in range(G - 1):
                add("sp", lambda e, r=r: e.dma_start(out=abc[:, :], in_=st_all[r * 128:(r + 1) * 128, 0:DI]), reads=[b_stall], writes=[b_abc], dma=True)
                add("sp", lambda e, r=r: e.dma_start(out=dr[:], in_=dt_all[r * 128:(r + 1) * 128, :]), reads=[b_dtall], writes=[b_dr], dma=True)
                add("dve", lambda e, r=r: e.tensor_scalar(out=ar[:], in0=dr[:], scalar1=mH[:, r:r + 1], scalar2=mHc[:, r:r + 1],
                                                           op0=ALU.mult, op1=ALU.add), reads=[b_dr, b_msk], writes=[b_ar])
                add("dve", lambda e: e.tensor_tensor(out=Sn[:], in0=Sn[:], in1=ar[:, :].unsqueeze(2).to_broadcast([128, 16, 128]), op=ALU.mult),
                    reads=[b_Sn, b_ar], writes=[b_Sn])
                add("dve", lambda e, r=r: e.scalar_tensor_tensor(out=Sn[:], in0=abc[:, :].rearrange("p (c n) -> p c n", c=16), scalar=mH[:, r:r + 1],
                                                                  in1=Sn[:], op0=ALU.mult, op1=ALU.add),
                    reads=[b_abc, b_msk, b_Sn], writes=[b_Sn])
            prompt_tiles(l, "p2")
            if l < L - 1:
                add("sp", lambda e: e.dma_start(out=xh_src.rearrange("p (c t) -> p c t", c=8), in_=Ureg[:, :, NT - NHL:NT]),
                    reads=b_U, writes=[b_xhsrc], dma=True)
                add("pool", lambda e: e.collective_compute("AllGather", ALU.bypass, replica_groups=grp, ins=[xh_src], outs=[xh_all]),
                    reads=[b_xhsrc], writes=[b_xhall], cc=True)
            if batched:
                if l < L - 1:
                    sample_tiles(l + 1, 0, n_samp)
            else:
                sample_tiles(l, h, n_samp)
            if l < L - 1:
                Gv = abc[:, :G * 8 * NHL].rearrange("p (r f) -> p r f", r=G)
                add("sp", lambda e: e.dma_start(out=Gv, in_=xh_all.rearrange("(r p) f -> p r f", p=128)), reads=[b_xhall], writes=[b_abc], dma=True)
                add("dve", lambda e: e.tensor_scalar_mul(out=xhf[:], in0=Gv[:, 0, :], scalar1=msel[:, 0:1]), reads=[b_abc, b_msk], writes=[b_xhf])
                for r in range(1, G):
                    add("dve", lambda e, r=r: e.scalar_tensor_tensor(out=xhf[:], in0=Gv[:, r, :], scalar=msel[:, r:r + 1], in1=xhf[:],
                                                                      op0=ALU.mult, op1=ALU.add), reads=[b_abc, b_msk, b_xhf], writes=[b_xhf])
                add("dve", lambda e: e.tensor_copy(out=xh_b[:, :, 0:NHL], in_=xhf[:].rearrange("p (c t) -> p c t", c=8)),
                    reads=[b_xhf], writes=[b_xh])
    except _Stop:
        pass

    run_engine, sems, last = S.emit(lambda n: es.enter_context(nc.semaphore(n)))
    with nc.Block() as block:
        @block.sync
        def _(e):
            run_engine("sp", e)
            for sn, v in last.items():
                e.wait_ge(sems[sn], v)

        @block.tensor
        def _(e):
            run_engine("pe", e)

        @block.scalar
        def _(e):
            run_engine("act", e)

        @block.vector
        def _(e):
            run_engine("dve", e)

        @block.gpsimd
        def _(e):
            run_engine("pool", e)
    es.close()
    return nc, len(S.ops)


def _chunked(v, L):
    C = v.shape[1]
    return np.ascontiguousarray(v.reshape(L, C // 128, 128).transpose(2, 0, 1))


def prep_shared(w, L):
    f = np.float32
    b_in = w["b_in"]
    cols = []
    for c0, n in ((C_VAL, 8), (C_GLU, 8), (C_GATE, 8), (C_Z, 16), (C_XBC, 24), (C_GA, 8), (C_GB, 8), (C_GP, 8)):
        cols.append(b_in[:, c0:c0 + n * 128])
    bm = np.concatenate(cols, axis=1)
    d = {
        "w_in": w["w_in"], "w_a_out": w["w_a_out"], "w_b_out": w["w_b_out"], "w_out": w["w_out"], "w_ple": w["w_ple"],
        "b_main": _chunked(bm, L),
        "b_dt": np.ascontiguousarray(b_in[:, C_DT:C_DT + NH]),
        "dt_bias": w["dt_bias"], "a_log": w["a_log"],
        "caw": np.ascontiguousarray(w["conv_a_w"].reshape(L, KA, 8, 128).transpose(3, 0, 2, 1)),
        "cab": _chunked(w["conv_a_b"], L), "nag": _chunked(w["norm_a_g"], L), "nab": _chunked(w["norm_a_b"], L),
        "cbw": np.ascontiguousarray(w["conv_b_w"].reshape(L, KB, 24, 128).transpose(3, 0, 2, 1)),
        "cbb": _chunked(w["conv_b_b"], L), "gnw": _chunked(w["gnorm_w"], L),
        "lng": _chunked(w["ln_g"], L), "lnb": _chunked(w["ln_b"], L),
        "dsk": _chunked(np.repeat(w["d_skip"], HP, axis=1), L),
        "c_ident": np.eye(128, dtype=f), "c_ones": np.ones((128, 128), f),
        "c_U": np.triu(np.ones((128, 128), f)), "c_Lw": np.tril(np.ones((128, 128), f), -1),
        "c_Ubd": np.triu(np.ones((128, 128), f)) * np.kron(np.eye(4, dtype=f), np.ones((32, 32), f)),
        "c_Lwbd": np.tril(np.ones((128, 128), f), -1) * np.kron(np.eye(4, dtype=f), np.ones((32, 32), f)),
        "c_sqm": np.kron(np.eye(4, dtype=f), np.ones((32, 1), f)),
    }
    return {k: np.ascontiguousarray(v, dtype=f) for k, v in d.items()}


_CACHE = {}


def run_sharded(inp, ncores=8, nseg=4, trace=False):
    f = np.float32
    x_prompt, x_sample, p_prompt, p_sample = inp["x_prompt"], inp["x_sample"], inp["p_prompt"], inp["p_sample"]
    cache_conv_a, cache_conv_b, state_ssm = inp["cache_conv_a"], inp["cache_conv_b"], inp["state_ssm"]
    L = inp["w_in"].shape[0]
    Bp, Tp, _ = x_prompt.shape
    Bs, Ts, _ = x_sample.shape
    assert Bp * nseg == ncores and Bs % ncores == 0 and Tp % (nseg * 512) == 0
    n_samp = Bs // ncores
    Tseg = Tp // nseg
    shared = prep_shared(inp, L)
    groups = tuple(tuple(range(b * nseg, (b + 1) * nseg)) for b in range(Bp))
    key = (Tseg, n_samp, Ts, L, groups)
    if key not in _CACHE:
        _CACHE[key] = build_program(Tseg, n_samp, Ts, L, seg=True, groups=groups)[0]
    nc = _CACHE[key]
    st = state_ssm.reshape(L, Bs, DI, NS)
    in_maps = []
    for c in range(ncores):
        b, k = divmod(c, nseg)
        t0 = k * Tseg
        sl = slice(c * n_samp, (c + 1) * n_samp)
        xprev = x_prompt[b, t0 - (KA - 1):t0] if k > 0 else np.zeros((KA - 1, D), f)
        msel = np.zeros((128, 4), f)
        if k > 0:
            msel[:, k - 1] = 1.0
        mH = np.zeros((128, 4), f)
        mH[:, :k] = 1.0
        m = dict(shared)
        m.update({
            "xp_in": x_prompt[b, t0:t0 + Tseg], "pp_in": p_prompt[:, b, t0:t0 + Tseg], "xprev": xprev,
            "hflag_in": np.full((128, 32), 1.0 if k > 0 else 0.0, f), "msel_in": msel, "mH_in": mH, "mHc_in": 1.0 - mH,
            "xs_in": x_sample[sl], "ps_in": p_sample[:, sl],
            "ca_in": cache_conv_a[:, sl], "cb_in": cache_conv_b[:, sl], "st_in": st[:, sl],
        })
        in_maps.append({kk: np.ascontiguousarray(v, dtype=f) for kk, v in m.items()})
    res = run_bass_kernel_spmd(nc, in_maps, core_ids=list(range(ncores)), **({"trace": True} if trace else {}))
    R = res.results
    y_prompt = np.stack([np.concatenate([R[b * nseg + k]["yp"] for k in range(nseg)], 0) for b in range(Bp)], 0)
    y_sample = np.concatenate([R[c]["ys"] for c in range(ncores)], 0)
    lastc = [b * nseg + nseg - 1 for b in range(Bp)]
    ca_p = np.stack([R[c]["cap_o"] for c in lastc], 1)
    cb_p = np.stack([R[c]["cbp_o"] for c in lastc], 1)
    st_p = np.stack([R[c]["stp_o"] for c in lastc], 1).reshape(L, Bp, NH, HP, NS)
    ca_s = np.concatenate([R[c]["cas_o"] for c in range(ncores)], 1)
    cb_s = np.concatenate([R[c]["cbs_o"] for c in range(ncores)], 1)
    st_s = np.concatenate([R[c]["sts_o"] for c in range(ncores)], 1).reshape(L, Bs, NH, HP, NS)
    out = (y_prompt, y_sample, ca_p, cb_p, st_p, ca_s, cb_s, st_s)
    return tuple(np.ascontiguousarray(o, dtype=f) for o in out), res


def kernel(x_prompt, x_sample, cache_conv_a, cache_conv_b, state_ssm, p_prompt, p_sample,
           w_in, b_in, conv_a_w, conv_a_b, norm_a_g, norm_a_b, w_a_out,
           conv_b_w, conv_b_b, dt_bias, a_log, d_skip, gnorm_w, w_b_out, w_out, w_ple, ln_g, ln_b):
    A = lambda v: np.asarray(v, dtype=np.float32)
    inp = dict(x_prompt=A(x_prompt), x_sample=A(x_sample), cache_conv_a=A(cache_conv_a), cache_conv_b=A(cache_conv_b),
               state_ssm=A(state_ssm), p_prompt=A(p_prompt), p_sample=A(p_sample),
               w_in=A(w_in), b_in=A(b_in), conv_a_w=A(conv_a_w), conv_a_b=A(conv_a_b), norm_a_g=A(norm_a_g), norm_a_b=A(norm_a_b),
               w_a_out=A(w_a_out), conv_b_w=A(conv_b_w), conv_b_b=A(conv_b_b), dt_bias=A(dt_bias), a_log=A(a_log), d_skip=A(d_skip),
               gnorm_w=A(gnorm_w), w_b_out=A(w_b_out), w_out=A(w_out), w_ple=A(w_ple), ln_g=A(ln_g), ln_b=A(ln_b))
    out, _ = run_sharded(inp, ncores=8, nseg=4)
    return out
```
